# Optimizing a Trainium2 kernel written in Bass

```python
import math, functools
import jax, jax.numpy as jnp
from jax import lax
import numpy as np

D_MODEL = 1024
BATCH = 8
SEQ = 4096
DEPTH = 2
DEC_BATCH = 8
DEC_SEQ = 64
PAST_LEN = 1024

CHUNK = 64
Q_BLOCK = 128
FOX_HEADS = 8
FOX_HEAD_DIM = 64
FOX_DIM = FOX_HEADS * FOX_HEAD_DIM
CONV_DIM = 512
CONV_WIDTH = 3
GMLP_CHUNK = 128
GMLP_GROUPS = 8
GMLP_DIM = D_MODEL
GMLP_GROUP_DIM = GMLP_DIM // GMLP_GROUPS
MEM_LEN = 256
MEM_HEADS = 4
MEM_HEAD_DIM = 128
MEM_DIM = MEM_HEADS * MEM_HEAD_DIM
D_FF = ((8 * D_MODEL + 3 * 256 - 1) // (3 * 256)) * 256
N_EVEN = (DEPTH + 1) // 2
N_ODD = DEPTH // 2
DEEPNORM_ALPHA = (2 * DEPTH) ** 0.25
DEEPNORM_BETA = (8 * DEPTH) ** -0.25
LN_EPS = 1e-5
EVEN_SPLITS = (FOX_DIM, 2 * FOX_DIM, 3 * FOX_DIM, 3 * FOX_DIM + FOX_HEADS,
               3 * FOX_DIM + FOX_HEADS + CONV_DIM, 3 * FOX_DIM + FOX_HEADS + 2 * CONV_DIM)
EVEN_IN_DIM = 3 * FOX_DIM + FOX_HEADS + 3 * CONV_DIM

kernel_name = 'fox_shortconv_gmlp_stream_step'


def layer_norm(x, g, b, out_dtype):
    xf = x.astype(jnp.float32)
    mu = jnp.mean(xf, axis=-1, keepdims=True)
    var = jnp.mean(jnp.square(xf - mu), axis=-1, keepdims=True)
    return ((xf - mu) * lax.rsqrt(var + LN_EPS) * g.astype(jnp.float32) + b.astype(jnp.float32)).astype(out_dtype)


def post_norm(x, y, g, b):
    return layer_norm(DEEPNORM_ALPHA * x.astype(jnp.float32) + y.astype(jnp.float32), g, b, x.dtype)


def even_projections(x, w_in, b_f):
    bn, t = x.shape[:2]
    z = x @ w_in
    q, k, v, f, h, bg, cg = jnp.split(z, EVEN_SPLITS, axis=-1)
    hs = (bn, t, FOX_HEADS, FOX_HEAD_DIM)
    logf = jax.nn.log_sigmoid(f.astype(jnp.float32) + b_f.astype(jnp.float32))
    return q.reshape(hs), k.reshape(hs), v.reshape(hs), logf, cg * h, bg


def fox_attention_prompt(q, k, v, logf):
    bn, s_len = q.shape[:2]
    scale = FOX_HEAD_DIM ** -0.5
    c_t = jnp.swapaxes(jnp.cumsum(logf, axis=1), 1, 2)
    kpos = jnp.arange(s_len)

    def block(i):
        start = i * Q_BLOCK
        qb = lax.dynamic_slice_in_dim(q, start, Q_BLOCK, axis=1)
        cq = lax.dynamic_slice_in_dim(c_t, start, Q_BLOCK, axis=2)
        s = jnp.einsum('bqhd,bkhd->bhqk', qb, k, preferred_element_type=jnp.float32) * scale
        s = s + (cq[..., :, None] - c_t[..., None, :])
        qpos = start + jnp.arange(Q_BLOCK)
        s = jnp.where(kpos[None, :] <= qpos[:, None], s, -jnp.inf)
        p = jax.nn.softmax(s, axis=-1)
        return jnp.einsum('bhqk,bkhd->bqhd', p.astype(v.dtype), v)

    out = lax.map(block, jnp.arange(s_len // Q_BLOCK))
    return jnp.swapaxes(out, 0, 1).reshape(bn, s_len, FOX_DIM)


def fox_attention_sample(q, k_new, v_new, logf_new, k_cache, v_cache, logf_cache):
    bn, t = q.shape[:2]
    p_len = k_cache.shape[1]
    scale = FOX_HEAD_DIM ** -0.5
    k = jnp.concatenate([k_cache.astype(k_new.dtype), k_new], axis=1)
    v = jnp.concatenate([v_cache.astype(v_new.dtype), v_new], axis=1)
    logf = jnp.concatenate([logf_cache.astype(jnp.float32), logf_new], axis=1)
    c_t = jnp.swapaxes(jnp.cumsum(logf, axis=1), 1, 2)
    cq = c_t[..., p_len:]
    s = jnp.einsum('bqhd,bkhd->bhqk', q, k, preferred_element_type=jnp.float32) * scale
    s = s + (cq[..., :, None] - c_t[..., None, :])
    kpos = jnp.arange(p_len + t)
    qpos = p_len + jnp.arange(t)
    s = jnp.where(kpos[None, :] <= qpos[:, None], s, -jnp.inf)
    p = jax.nn.softmax(s, axis=-1)
    return jnp.einsum('bhqk,bkhd->bqhd', p.astype(v.dtype), v).reshape(bn, t, FOX_DIM)


def causal_short_conv(x_hist, w):
    return lax.conv_general_dilated(x_hist, w[:, None, :].astype(x_hist.dtype), (1,), 'VALID',
                                    dimension_numbers=('NWC', 'WIO', 'NWC'),
                                    feature_group_count=x_hist.shape[-1])


def gmlp_gate_inputs(x, w_in, b_in, g, b):
    z = jax.nn.gelu(x @ w_in + b_in, approximate=False)
    u, v = jnp.split(z, 2, axis=-1)
    return u, layer_norm(v, g, b, v.dtype)


def gmlp_spatial_prompt(v, w_s, b_s):
    bn, s_len = v.shape[:2]
    vb = v.reshape(bn, s_len // GMLP_CHUNK, GMLP_CHUNK, GMLP_GROUPS, GMLP_GROUP_DIM)
    w = w_s * jnp.tril(jnp.ones((GMLP_CHUNK, GMLP_CHUNK), w_s.dtype))
    mixed = jnp.einsum('gts,bnsgc->bntgc', w, vb) + b_s.T[None, None, :, :, None]
    return mixed.reshape(bn, s_len, GMLP_DIM)


def gmlp_spatial_sample(v, w_s, b_s):
    bn, t = v.shape[:2]
    w = (w_s * jnp.tril(jnp.ones((GMLP_CHUNK, GMLP_CHUNK), w_s.dtype)))[:, :t, :t]
    vb = v.reshape(bn, t, GMLP_GROUPS, GMLP_GROUP_DIM)
    mixed = jnp.einsum('gts,bsgc->btgc', w, vb) + b_s[:, :t].T[None, :, :, None]
    return mixed.reshape(bn, t, GMLP_DIM)


def memory_kv(mem, w_k, w_v):
    bn, m = mem.shape[:2]
    hs = (bn, m, MEM_HEADS, MEM_HEAD_DIM)
    return (mem @ w_k).reshape(hs), (mem @ w_v).reshape(hs)


def memory_attention(x, mk, mv, w_q, w_o):
    bn, t = x.shape[:2]
    q = (x @ w_q).reshape(bn, t, MEM_HEADS, MEM_HEAD_DIM)
    s = jnp.einsum('bthd,bmhd->bhtm', q, mk.astype(q.dtype), preferred_element_type=jnp.float32) * (MEM_HEAD_DIM ** -0.5)
    p = jax.nn.softmax(s, axis=-1)
    o = jnp.einsum('bhtm,bmhd->bthd', p.astype(q.dtype), mv.astype(q.dtype))
    return o.reshape(bn, t, MEM_DIM) @ w_o


def swiglu(x, w_gate, w_up, w_down):
    return (jax.nn.silu(x @ w_gate) * (x @ w_up)) @ w_down


def setup_inputs(seed: int = 0) -> dict:
    key = jax.random.key(seed)
    ks = iter(jax.random.split(key, 40))
    nrm = lambda shape, s=1.0: jax.random.normal(next(ks), shape, jnp.float32) * s
    d = D_MODEL
    return {
        'x_prompt': nrm((BATCH, SEQ, d)),
        'x_sample': nrm((DEC_BATCH, DEC_SEQ, d)),
        'cache_fox_k': nrm((N_EVEN, DEC_BATCH, PAST_LEN, FOX_HEADS, FOX_HEAD_DIM)),
        'cache_fox_v': nrm((N_EVEN, DEC_BATCH, PAST_LEN, FOX_HEADS, FOX_HEAD_DIM)),
        'cache_fox_logf': jax.nn.log_sigmoid(nrm((N_EVEN, DEC_BATCH, PAST_LEN, FOX_HEADS)) + 2.5),
        'state_conv': nrm((N_EVEN, DEC_BATCH, CONV_WIDTH - 1, CONV_DIM), 0.5),
        'cache_mem_k': nrm((DEPTH, DEC_BATCH, MEM_LEN, MEM_HEADS, MEM_HEAD_DIM)),
        'cache_mem_v': nrm((DEPTH, DEC_BATCH, MEM_LEN, MEM_HEADS, MEM_HEAD_DIM)),
        'mem_prompt': nrm((BATCH, MEM_LEN, d)),
        'w_in_even': nrm((N_EVEN, d, EVEN_IN_DIM), d ** -0.5),
        'b_forget': jax.random.uniform(next(ks), (N_EVEN, FOX_HEADS), jnp.float32, 1.0, 4.0),
        'conv_w': nrm((N_EVEN, CONV_WIDTH, CONV_DIM), CONV_WIDTH ** -0.5),
        'w_out_even': nrm((N_EVEN, FOX_DIM + CONV_DIM, d), (FOX_DIM + CONV_DIM) ** -0.5 * DEEPNORM_BETA),
        'w_in_odd': nrm((N_ODD, d, 2 * GMLP_DIM), d ** -0.5),
        'b_in_odd': nrm((N_ODD, 2 * GMLP_DIM), 0.02),
        'gmlp_norm_g': 1.0 + nrm((N_ODD, GMLP_DIM), 0.02),
        'gmlp_norm_b': nrm((N_ODD, GMLP_DIM), 0.02),
        'gmlp_w_s': nrm((N_ODD, GMLP_GROUPS, GMLP_CHUNK, GMLP_CHUNK), GMLP_CHUNK ** -0.5),
        'gmlp_b_s': 1.0 + nrm((N_ODD, GMLP_GROUPS, GMLP_CHUNK), 0.02),
        'w_out_odd': nrm((N_ODD, GMLP_DIM, d), GMLP_DIM ** -0.5 * DEEPNORM_BETA),
        'mem_w_q': nrm((DEPTH, d, MEM_DIM), d ** -0.5),
        'mem_w_k': nrm((DEPTH, d, MEM_DIM), d ** -0.5),
        'mem_w_v': nrm((DEPTH, d, MEM_DIM), d ** -0.5),
        'mem_w_o': nrm((DEPTH, MEM_DIM, d), MEM_DIM ** -0.5 * DEEPNORM_BETA),
        'ffn_w_gate': nrm((DEPTH, d, D_FF), d ** -0.5),
        'ffn_w_up': nrm((DEPTH, d, D_FF), d ** -0.5),
        'ffn_w_down': nrm((DEPTH, D_FF, d), D_FF ** -0.5 * DEEPNORM_BETA),
        'ln_g': 1.0 + nrm((DEPTH, 3, d), 0.02),
        'ln_b': nrm((DEPTH, 3, d), 0.02),
    }


def reference(x_prompt, x_sample, cache_fox_k, cache_fox_v, cache_fox_logf, state_conv,
              cache_mem_k, cache_mem_v, mem_prompt,
              w_in_even, b_forget, conv_w, w_out_even,
              w_in_odd, b_in_odd, gmlp_norm_g, gmlp_norm_b, gmlp_w_s, gmlp_b_s, w_out_odd,
              mem_w_q, mem_w_k, mem_w_v, mem_w_o,
              ffn_w_gate, ffn_w_up, ffn_w_down, ln_g, ln_b):
    fk_p, fv_p, fl_p, cs_p, mk_p, mv_p = [], [], [], [], [], []
    fk_s, fv_s, fl_s, cs_s, gv_s = [], [], [], [], []
    yp, ys = x_prompt, x_sample
    for layer in range(DEPTH):
        if layer % 2 == 0:
            e = layer // 2
            q, k, v, logf, pre, bg = even_projections(yp, w_in_even[e], b_forget[e])
            att = fox_attention_prompt(q, k, v, logf)
            hist = jnp.pad(pre, ((0, 0), (CONV_WIDTH - 1, 0), (0, 0)))
            conv = bg * causal_short_conv(hist, conv_w[e])
            mix_p = jnp.concatenate([att, conv], axis=-1) @ w_out_even[e]
            fk_p.append(k)
            fv_p.append(v)
            fl_p.append(logf)
            cs_p.append(hist[:, -(CONV_WIDTH - 1):])
            q, k, v, logf, pre, bg = even_projections(ys, w_in_even[e], b_forget[e])
            att = fox_attention_sample(q, k, v, logf, cache_fox_k[e], cache_fox_v[e], cache_fox_logf[e])
            hist = jnp.concatenate([state_conv[e].astype(pre.dtype), pre], axis=1)
            conv = bg * causal_short_conv(hist, conv_w[e])
            mix_s = jnp.concatenate([att, conv], axis=-1) @ w_out_even[e]
            fk_s.append(k)
            fv_s.append(v)
            fl_s.append(logf)
            cs_s.append(hist[:, -(CONV_WIDTH - 1):])
        else:
            o = layer // 2
            u, vn = gmlp_gate_inputs(yp, w_in_odd[o], b_in_odd[o], gmlp_norm_g[o], gmlp_norm_b[o])
            mix_p = (u * gmlp_spatial_prompt(vn, gmlp_w_s[o], gmlp_b_s[o])) @ w_out_odd[o]
            u, vn = gmlp_gate_inputs(ys, w_in_odd[o], b_in_odd[o], gmlp_norm_g[o], gmlp_norm_b[o])
            mix_s = (u * gmlp_spatial_sample(vn, gmlp_w_s[o], gmlp_b_s[o])) @ w_out_odd[o]
            gv_s.append(vn)
        yp = post_norm(yp, mix_p, ln_g[layer, 0], ln_b[layer, 0])
        ys = post_norm(ys, mix_s, ln_g[layer, 0], ln_b[layer, 0])
        mk, mv = memory_kv(mem_prompt, mem_w_k[layer], mem_w_v[layer])
        mk_p.append(mk)
        mv_p.append(mv)
        yp = post_norm(yp, memory_attention(yp, mk, mv, mem_w_q[layer], mem_w_o[layer]), ln_g[layer, 1], ln_b[layer, 1])
        ys = post_norm(ys, memory_attention(ys, cache_mem_k[layer], cache_mem_v[layer], mem_w_q[layer], mem_w_o[layer]), ln_g[layer, 1], ln_b[layer, 1])
        yp = post_norm(yp, swiglu(yp, ffn_w_gate[layer], ffn_w_up[layer], ffn_w_down[layer]), ln_g[layer, 2], ln_b[layer, 2])
        ys = post_norm(ys, swiglu(ys, ffn_w_gate[layer], ffn_w_up[layer], ffn_w_down[layer]), ln_g[layer, 2], ln_b[layer, 2])
    fox_k_prompt = jnp.stack(fk_p)
    fox_v_prompt = jnp.stack(fv_p)
    fox_logf_prompt = jnp.stack(fl_p)
    conv_state_prompt = jnp.stack(cs_p)
    mem_k_prompt = jnp.stack(mk_p)
    mem_v_prompt = jnp.stack(mv_p)
    fox_k_sample = jnp.stack(fk_s)
    fox_v_sample = jnp.stack(fv_s)
    fox_logf_sample = jnp.stack(fl_s)
    conv_state_sample = jnp.stack(cs_s)
    gmlp_v_sample = jnp.stack(gv_s)
    return (yp, ys, fox_k_prompt, fox_v_prompt, fox_logf_prompt, conv_state_prompt, mem_k_prompt, mem_v_prompt,
            fox_k_sample, fox_v_sample, fox_logf_sample, conv_state_sample, gmlp_v_sample)
```

```python
import contextlib
import numpy as np
import concourse.bass as bass
import concourse.mybir as mybir
from concourse.bass_utils import run_bass_kernel_spmd

F32 = mybir.dt.float32
BF16 = mybir.dt.bfloat16
AF = mybir.ActivationFunctionType
ALU = mybir.AluOpType

ENGS = ('pe', 'act', 'dve', 'pool', 'sp')

NPROMPT_TILES = 8
DO_SAMPLE = True
NSLOT = 4
ALPHA = float((2 * 2) ** 0.25)
EPS = 1e-5


class Op:
    __slots__ = ('eng', 'fn', 'reads', 'writes', 'dma', 'idx', 'eidx', 'waits', 'marked', 'dmaval', 'clock')


class Prog:
    def __init__(self, nc):
        self.nc = nc
        self.ops = []
        self.dma_counts = {}

    def op(self, eng, fn, reads=(), writes=(), dma=None):
        o = Op()
        o.eng = eng; o.fn = fn; o.reads = tuple(reads); o.writes = tuple(writes); o.dma = dma
        o.idx = len(self.ops); o.waits = []; o.marked = False; o.dmaval = None
        if dma is not None:
            self.dma_counts[dma] = self.dma_counts.get(dma, 0) + 16
            o.dmaval = self.dma_counts[dma]
        self.ops.append(o)
        return o

    def pe(self, fn, r=(), w=()): return self.op('pe', fn, r, w)
    def act(self, fn, r=(), w=()): return self.op('act', fn, r, w)
    def dve(self, fn, r=(), w=()): return self.op('dve', fn, r, w)
    def pool(self, fn, r=(), w=()): return self.op('pool', fn, r, w)
    def dma(self, q, sem, fn, r=(), w=()): return self.op(q, fn, r, w, dma=sem)

    def op_at(self, pos, eng, fn, reads=(), writes=(), dma=None):
        o = self.op(eng, fn, reads, writes, dma)
        self.ops.pop()
        self.ops.insert(pos, o)
        return o

    def fence(self, src_keys, dst_keys):
        self.ops.append(('fence', tuple(src_keys), tuple(dst_keys)))

    def _semkey(self, o): return ('d', o.dma) if o.dma is not None else ('e', o.eng)
    def _semval(self, o): return o.dmaval if o.dma is not None else o.eidx

    def finalize(self, out_dma_sems=()):
        raw = self.ops
        ops = [o for o in raw if isinstance(o, Op)]
        for n, o in enumerate(ops):
            o.idx = n
        ecount = {e: 0 for e in ENGS}
        for o in ops:
            if o.dma is None:
                ecount[o.eng] += 1
                o.eidx = ecount[o.eng]
            else:
                o.eidx = None
        writers = {}
        readers = {}
        eng_clock = {e: {} for e in ENGS}
        ps_last = {}
        for o in raw:
            if not isinstance(o, Op):
                _, src, dst = o
                ws = []; rs = []
                for k in src:
                    ws += writers.get(k, []); rs += readers.get(k, [])
                for k in dst:
                    writers[k] = writers.get(k, []) + ws
                    readers[k] = readers.get(k, []) + rs
                continue
            deps = []
            for k in o.reads:
                for i in writers.get(k, ()):
                    deps.append((i, 'raw'))
            for k in o.writes:
                for i in writers.get(k, ()):
                    deps.append((i, 'waw'))
                for i in readers.get(k, ()):
                    deps.append((i, 'war'))
            for k in set(o.reads) | set(o.writes):
                if isinstance(k, tuple) and k and k[0] == 'ps':
                    d = ps_last.setdefault(k, {})
                    for en, i in d.items():
                        if en != o.eng:
                            deps.append((i, 'x'))
                    d[o.eng] = o.idx
            clock = eng_clock[o.eng]
            need = {}
            for (i, kind) in deps:
                p = ops[i]
                if p is o:
                    continue
                if p.dma is None and p.eng == o.eng:
                    if o.eng in ('pe', 'sp'):
                        continue
                sk = self._semkey(p); sv = self._semval(p)
                if clock.get(sk, 0) >= sv:
                    continue
                if need.get(sk, (0, None))[0] < sv:
                    need[sk] = (sv, p)
            for sk, (sv, p) in need.items():
                if clock.get(sk, 0) >= sv:
                    continue
                o.waits.append(p)
                p.marked = True
                for k2, v2 in p.clock.items():
                    if clock.get(k2, 0) < v2:
                        clock[k2] = v2
                if clock.get(sk, 0) < sv:
                    clock[sk] = sv
            o.clock = dict(clock)
            o.clock[self._semkey(o)] = self._semval(o)
            if o.dma is None and o.eng == 'pe':
                clock[('e', 'pe')] = o.eidx
            for k in o.writes:
                writers[k] = [o.idx]
                readers[k] = []
            for k in o.reads:
                readers.setdefault(k, []).append(o.idx)
        self.inc_no = {}
        cnt = {e: 0 for e in ENGS}
        for o in ops:
            if o.dma is None and o.marked:
                cnt[o.eng] += 1
                self.inc_no[o.idx] = cnt[o.eng]
        self.out_dma_sems = tuple(out_dma_sems)
        self.ops = ops

    def emit(self):
        nc = self.nc
        with contextlib.ExitStack() as es:
            esem = {e: es.enter_context(nc.semaphore('s_' + e)) for e in ENGS}
            dsem = {k: es.enter_context(nc.semaphore('d_' + str(k))) for k in self.dma_counts}
            block = es.enter_context(nc.Block())
            per = {e: [o for o in self.ops if o.eng == e] for e in ENGS}

            def run(eng_name, eng):
                for o in per[eng_name]:
                    for p in o.waits:
                        if p.dma is not None:
                            eng.wait_ge(dsem[p.dma], p.dmaval)
                        else:
                            eng.wait_ge(esem[p.eng], self.inc_no[p.idx])
                    ins = o.fn(eng)
                    if o.dma is not None:
                        ins.then_inc(dsem[o.dma], 16)
                    elif o.marked:
                        ins.then_inc(esem[o.eng], 1)
                if eng_name == 'sp':
                    for k in self.out_dma_sems:
                        eng.wait_ge(dsem[k], self.dma_counts[k])

            @block.tensor
            def _(e): run('pe', e)

            @block.scalar
            def _(e): run('act', e)

            @block.vector
            def _(e): run('dve', e)

            @block.gpsimd
            def _(e): run('pool', e)

            @block.sync
            def _(e): run('sp', e)


WSPEC = {
    'win0': (1024, 3080), 'wout0': (1024, 1024), 'win1': (1024, 2048), 'wout1': (1024, 1024),
}
for _l in range(2):
    WSPEC[f'mq{_l}'] = (1024, 512); WSPEC[f'mk{_l}'] = (1024, 512); WSPEC[f'mv{_l}'] = (1024, 512)
    WSPEC[f'mo{_l}'] = (512, 1024)
    WSPEC[f'fg{_l}'] = (1024, 2816); WSPEC[f'fu{_l}'] = (1024, 2816); WSPEC[f'fd{_l}'] = (2816, 1024)

WORK_BYTES = 34816
PAGE = 1024


def build_program(nprompt=NPROMPT_TILES, do_sample=DO_SAMPLE, dbg=99):
    nc = bass.Bass("TRN2", target_bir_lowering=False)
    S = 512 * nprompt
    din = lambda name, shape: nc.dram_tensor(name, list(shape), F32, kind="ExternalInput").ap()
    dout = lambda name, shape: nc.dram_tensor(name, list(shape), F32, kind="ExternalOutput").ap()
    xp = din('xp', (4096, 1024)); xs = din('xs', (64, 1024))
    cfk = din('cfk', (1024, 512)); cfv = din('cfv', (1024, 512)); cfl = din('cfl', (1024, 8))
    sconv = din('sconv', (2, 512))
    cmk = din('cmk', (2, 256, 512)); cmv = din('cmv', (2, 256, 512))
    memp = din('memp', (256, 1024))
    W32 = {n: din(n, WSPEC[n]) for n in WSPEC}
    b_forget = din('b_forget', (1, 8)); conv_w = din('conv_w', (12, 128))
    b_in_odd = din('b_in_odd', (16, 128)); gng = din('gng', (8, 128)); gnb = din('gnb', (8, 128))
    gws = din('gws', (8, 128, 128)); gbs = din('gbs', (1, 1024))
    ln_g = din('ln_g', (48, 128)); ln_b = din('ln_b', (48, 128))
    yp = dout('yp', (4096, 1024)); ys = dout('ys', (64, 1024))
    fkp = dout('fkp', (4096, 512)); fvp = dout('fvp', (4096, 512)); flp = dout('flp', (4096, 8))
    csp = dout('csp', (2, 512)); mkp = dout('mkp', (2, 256, 512)); mvp = dout('mvp', (2, 256, 512))
    fks = dout('fks', (64, 512)); fvs = dout('fvs', (64, 512)); fls = dout('fls', (64, 8))
    css = dout('css', (2, 512)); gvs = dout('gvs', (64, 1024))
    WS = {n: nc.dram_tensor('ws_' + n, [16, 128, 2048], BF16, kind="Internal").ap() for n in WSPEC if n[:2] not in ('mk', 'mv')}

    P = Prog(nc)
    es = contextlib.ExitStack()
    with es:
        sb = lambda name, shape, dt: es.enter_context(nc.sbuf_tensor(name, list(shape), dt))
        ident = sb('ident', (128, 128), F32); onesf = sb('onesf', (128, 128), F32)
        tri = sb('tri', (128, 128), F32); ones_b = sb('ones_b', (128, 128), BF16)
        mask_b = sb('mask_b', (128, 128), BF16)
        colsA = sb('colsA', (128, 128), F32); colsB = sb('colsB', (128, 128), F32)
        bfb = sb('bfb', (128, 8), F32); bsb = sb('bsb', (128, 8, 128), F32)
        wsT = sb('wsT', (128, 8, 128), BF16)
        memKT = [sb(f'memKT{l}', (128, 4, 256), BF16) for l in range(2)]
        memV = [sb(f'memV{l}', (128, 2, 512), BF16) for l in range(2)]
        xT = sb('xT', (128, 8, 512), F32); xbf = sb('xbf', (128, 8, 512), BF16)
        KT = sb('KT', (128, 4, 4096), BF16); Vs = sb('Vs', (128, 32, 4, 192), BF16)
        ctok = sb('ctok', (128, 33, 8), F32); carry = sb('carry', (128, 34, 8), F32)
        lf = sb('lf', (128, 33, 8), F32); biasb = sb('biasb', (128, 33, 8), F32)
        ctok_s = sb('ctok_s', (128, 9, 8), F32); carry_s = sb('carry_s', (128, 10, 8), F32); lf_s = sb('lf_s', (128, 9, 8), F32)
        lft = sb('lft', (128, 2, 8), F32)
        T2 = sb('T2', (128, 4, 8), F32)
        dmy = sb('dmy', (128, 8), F32)
        wring = [sb(f'wr{i}', (128, 2048), BF16) for i in range(NSLOT)]
        xin = [sb(f'xin{i}', (128, 1024), F32) for i in range(2)]
        stg = [sb(f'stg{i}', (128, 512), F32) for i in range(2)]
        rbuf = [sb(f'rb{i}', (128, 512), BF16) for i in range(2)]
        rsq = [sb(f'rs{i}', (128, 512), BF16) for i in range(2)]
        Pb = [sb(f'Pb{i}', (128, 512), BF16) for i in range(4)]
        eight_b = sb('eight_b', (128, 128), BF16)
        inv_b = sb('inv_b', (128, 128), BF16)
        rcb = [sb(f'rc{i}', (128, 128), F32) for i in range(2)]
        prebuf = sb('prebuf', (128, 4, 516), F32)
        work = sb('work', (128, WORK_BYTES // 2), BF16)
        PS = [es.enter_context(nc.psum_tensor(f'ps{i}', [128, 512], F32)) for i in range(8)]

        def wv(off, shape, dt):
            n = int(np.prod(shape))
            if dt == BF16:
                a = work[:, off // 2: off // 2 + n]
            else:
                a = work[:, off // 2: off // 2 + 2 * n].bitcast(F32)
            if len(shape) == 2:
                a = a.rearrange("p (a b) -> p a b", a=shape[0])
            return a

        def pg(off, nbytes):
            return [('wk', i) for i in range(off // PAGE, (off + nbytes - 1) // PAGE + 1)]

        psk = lambda b: ('ps', b)

        wstate = {'n': 0, 'stg': 0, 'prev_start': None, 'cast': 0}
        wseen = {}
        NSTG = 4
        vflat = Vs[:, :, :, :].rearrange("p a c d -> p (a c d)")
        wstg = [vflat[:, 4096 + i * 4096: 4096 + (i + 1) * 4096].bitcast(F32) for i in range(NSTG)]

        def wblock(name, c0, cw, k0=0, nk=None, keep=True):
            K, M = WSPEC[name]
            if nk is None:
                nk = K // 128
            n = nk * cw
            i = wstate['n'] % NSLOT
            wstate['n'] += 1
            dst = wring[i][:, 0:n].rearrange("p (a b) -> p a b", a=nk)
            bkey = (name, c0, k0)
            if bkey in wseen:
                bid = wseen[bkey]
                src = WS[name][bid, :, 0:n].rearrange("p (a b) -> p a b", a=nk)
                P.dma('sp', f'w{i}', lambda e: e.dma_start(out=dst, in_=src), r=[('scr', name, bid)], w=[('w', i)])
                wstate['prev_start'] = None
                return dst, ('w', i)
            bid = len([k for k in wseen if k[0] == name])
            wseen[bkey] = bid
            g = wstate['stg'] % NSTG
            wstate['stg'] += 1
            stage = wstg[g][:, 0:n]
            src = W32[name].rearrange("(kc p) m -> p kc m", p=128)[:, k0:k0 + nk, c0:c0 + cw]
            pos = wstate['prev_start']
            natural = len(P.ops)
            if pos is None:
                pos = natural
            ceng = 'dve'
            wstate['cast'] += 1
            P.op_at(pos, 'sp', lambda e: e.dma_start(out=stage.rearrange("p (a b) -> p a b", a=nk), in_=src), (), [('wstg', g)], dma=f'ws{g}')
            if ceng == 'dve':
                P.op_at(pos + 1, 'dve', lambda e: e.tensor_copy(out=wring[i][:, 0:n], in_=stage), [('wstg', g)], [('w', i)])
            else:
                P.op_at(pos + 1, 'act', lambda e: e.copy(out=wring[i][:, 0:n], in_=stage), [('wstg', g)], [('w', i)])
            nins = 2
            if keep:
                P.op_at(pos + 2, 'pool', lambda e: e.dma_start(out=WS[name][bid, :, 0:n], in_=wring[i][:, 0:n]), [('w', i)], [('scr', name, bid)], dma=f'wb{i}')
                nins = 3
            wstate['prev_start'] = natural + nins
            return dst, ('w', i)

        P.pool(lambda e: e.memset(onesf[:], 1.0), w=['onesf'])
        P.pool(lambda e: e.memset(ones_b[:], 1.0), w=['ones_b'])
        P.pool(lambda e: e.memset(eight_b[:], 0.0), w=['eight_b'])
        P.pool(lambda e: e.memset(eight_b[0:1, :], 8.0), r=['eight_b'], w=['eight_b'])
        P.pool(lambda e: e.memset(inv_b[:], 1.0 / 1024), w=['inv_b'])
        P.pool(lambda e: e.memset(dmy[:], 1.0), w=['dmy'])
        P.pool(lambda e: e.affine_select(out=ident[:], in_=onesf[:], pattern=[[-1, 128]], compare_op=ALU.is_equal,
                                         fill=0.0, base=0, channel_multiplier=1), r=['onesf'], w=['ident'])
        P.pool(lambda e: e.affine_select(out=tri[:], in_=onesf[:], pattern=[[1, 128]], compare_op=ALU.is_ge,
                                         fill=0.0, base=0, channel_multiplier=-1), r=['onesf'], w=['tri'])
        P.pool(lambda e: e.affine_select(out=mask_b[:], in_=ones_b[:], pattern=[[1, 128]], compare_op=ALU.is_ge,
                                         fill=0.0, base=0, channel_multiplier=-1), r=['ones_b'], w=['mask_b'])
        P.pool(lambda e: e.memset(carry[:, 0, :], 0.0), w=[('carryp', 0)])
        P.pool(lambda e: e.memset(carry_s[:, 0, :], 0.0), w=[('carrys', 0)])
        P.pool(lambda e: e.memset(prebuf[:, :, 0:2], 0.0), w=['prebuf'])
        P.pool(lambda e: e.memset(xin[0][:, 0:128], 0.0), w=['xin0'])
        P.pool(lambda e: e.memset(xin[1][:, 0:128], 0.0), w=['xin1'])
        P.dma('sp', 'xi0', lambda e: e.dma_start(out=xin[0][0:48, 0:128], in_=ln_g[:, :]), r=['xin0'], w=['xin0'])
        P.dma('sp', 'xi0', lambda e: e.dma_start(out=xin[0][48:96, 0:128], in_=ln_b[:, :]), r=['xin0'], w=['xin0'])
        P.dma('sp', 'xi1', lambda e: e.dma_start(out=xin[1][0:16, 0:128], in_=b_in_odd[:, :]), r=['xin1'], w=['xin1'])
        P.dma('sp', 'xi1', lambda e: e.dma_start(out=xin[1][16:24, 0:128], in_=gng[:, :]), r=['xin1'], w=['xin1'])
        P.dma('sp', 'xi1', lambda e: e.dma_start(out=xin[1][24:32, 0:128], in_=gnb[:, :]), r=['xin1'], w=['xin1'])
        P.dma('sp', 'xi1', lambda e: e.dma_start(out=xin[1][32:44, 0:128], in_=conv_w[:, :]), r=['xin1'], w=['xin1'])
        P.dma('sp', 'c_bfb', lambda e: e.dma_start(out=bfb[:], in_=b_forget[0:1, :].partition_broadcast(128)), w=['bfb'])
        P.dma('sp', 'c_bsb', lambda e: e.dma_start(out=bsb[:].rearrange("p a b -> p (a b)"), in_=gbs[0:1, :].partition_broadcast(128)), w=['bsb'])
        P.pe(lambda e: e.transpose(out=PS[0][:, 0:128], in_=xin[0][:, 0:128], identity=ident[:]), r=['xin0', 'ident'], w=[psk(0)])
        P.pe(lambda e: e.transpose(out=PS[1][:, 0:128], in_=xin[1][:, 0:128], identity=ident[:]), r=['xin1', 'ident'], w=[psk(1)])
        P.act(lambda e: e.copy(out=colsA[:], in_=PS[0][:, 0:128]), r=[psk(0)], w=['colsA'])
        P.act(lambda e: e.copy(out=colsB[:], in_=PS[1][:, 0:128]), r=[psk(1)], w=['colsB'])
        lng = lambda l, k, c: colsA[:, (l * 3 + k) * 8 + c:(l * 3 + k) * 8 + c + 1]
        lnb = lambda l, k, c: colsA[:, 48 + (l * 3 + k) * 8 + c:48 + (l * 3 + k) * 8 + c + 1]
        bincol = lambda c: colsB[:, c:c + 1]
        gngcol = lambda c: colsB[:, 16 + c:17 + c]
        gnbcol = lambda c: colsB[:, 24 + c:25 + c]
        cwcol = lambda j, cc: colsB[:, 32 + j * 4 + cc:33 + j * 4 + cc]
        for g in range(8):
            st = xin[g % 2]
            k = f'xin{g % 2}'
            P.dma('sp', f'xi{g % 2}', lambda e, st=st, g=g: e.dma_start(out=st[:, 0:128], in_=gws[g, :, :]), w=[k])
            P.pool(lambda e, st=st: e.affine_select(out=st[:, 0:128], in_=st[:, 0:128], pattern=[[-1, 128]], compare_op=ALU.is_ge,
                                                     fill=0.0, base=0, channel_multiplier=1), r=[k], w=[k])
            P.pe(lambda e, st=st, g=g: e.transpose(out=PS[g % 2][:, 0:128], in_=st[:, 0:128], identity=ident[:]), r=[k, 'ident'], w=[psk(g % 2)])
            P.act(lambda e, g=g: e.copy(out=wsT[:, g, :], in_=PS[g % 2][:, 0:128]), r=[psk(g % 2)], w=['wsT'])

        mst = wv(0, (2, 1024), F32)
        memT = wv(8192, (8, 256), BF16)
        P.dma('sp', 'c_mst', lambda e: e.dma_start(out=mst, in_=memp.rearrange("(a p) d -> p a d", p=128)), w=pg(0, 8192))
        for mb in range(2):
            for half in range(2):
                bank = (mb * 2 + half) % 2
                for q in range(4):
                    c = half * 4 + q
                    P.pe(lambda e, mb=mb, c=c, q=q, bank=bank: e.transpose(out=PS[bank][:, q * 128:(q + 1) * 128], in_=mst[:, mb, c * 128:(c + 1) * 128], identity=ident[:]),
                         r=pg(0, 8192) + ['ident'], w=[psk(bank)])
                P.act(lambda e, mb=mb, half=half, bank=bank: e.copy(out=memT[:, half * 4:half * 4 + 4, mb * 128:(mb + 1) * 128],
                                                                      in_=PS[bank][:, :].rearrange("p (a b) -> p a b", a=4)),
                      r=[psk(bank)], w=pg(8192, 4096))
        mrk = pg(8192, 4096)
        stq = {'n': 0}

        def stage_out(ps_ap, rows, cols, dram_ap, psr, extra=None):
            i = stq['n'] % 2
            stq['n'] += 1
            P.act(lambda e: e.copy(out=stg[i][0:rows, 0:cols], in_=ps_ap), r=psr, w=[('stg', i)])
            P.dma('act', f'so{i}', lambda e: e.dma_start(out=dram_ap, in_=stg[i][0:rows, 0:cols]), r=[('stg', i)])

        for l in range(2):
            for nm, dst in (('mk', mkp), ('mv', mvp)):
                blks = [wblock(f'{nm}{l}', j * 256, 256, keep=False) for j in range(2)]
                if nm == 'mk':
                    for h in range(4):
                        wvw, wk = blks[h // 2]
                        bank = h % 2
                        for kc in range(8):
                            P.pe(lambda e, wvw=wvw, kc=kc, h=h, bank=bank: e.matmul(PS[bank][:, 0:256], lhsT=wvw[:, kc, (h % 2) * 128:(h % 2) * 128 + 128],
                                                                                     rhs=memT[:, kc, :], start=(kc == 0), stop=(kc == 7)),
                                 r=[wk] + mrk, w=[psk(bank)])
                        P.act(lambda e, h=h, bank=bank, l=l: e.copy(out=memKT[l][:, h, :], in_=PS[bank][:, 0:256]), r=[psk(bank)], w=[('memKT', l)])
                for mb in range(2):
                    bank = 2 + mb
                    for j in range(2):
                        wvw, wk = blks[j]
                        for kc in range(8):
                            P.pe(lambda e, wvw=wvw, kc=kc, mb=mb, j=j, bank=bank: e.matmul(PS[bank][:, j * 256:(j + 1) * 256], lhsT=memT[:, kc, mb * 128:(mb + 1) * 128],
                                                                                            rhs=wvw[:, kc, :], start=(kc == 0), stop=(kc == 7)),
                                 r=[wk] + mrk, w=[psk(bank)])
                    if nm == 'mv':
                        P.dve(lambda e, mb=mb, bank=bank, l=l: e.tensor_copy(out=memV[l][:, mb, :], in_=PS[bank][:, :]), r=[psk(bank)], w=[('memV', l)])
                    stage_out(PS[bank][:, :], 128, 512, dst[l, mb * 128:(mb + 1) * 128, :], [psk(bank)])

        mmq = {'n': 0}

        def mmbank():
            b = mmq['n'] % 4
            mmq['n'] += 1
            return b

        def ln_apply(NT, src_of, dstbf_of, gcol, bcol, src_keys, defer=True):
            o_tmp, o_A = 24576, 26624
            tmp = wv(o_tmp, (512,), F32); A = wv(o_A, (512,), F32)
            P.act(lambda e: e.activation(out=tmp[:, 0:NT], in_=PS[6][:, 0:NT], func=AF.Square), r=[psk(6)], w=pg(o_tmp, 2048))
            P.dve(lambda e: e.tensor_tensor(out=tmp[:, 0:NT], in0=PS[7][:, 0:NT], in1=tmp[:, 0:NT], op=ALU.subtract), r=[psk(7)] + pg(o_tmp, 2048), w=pg(o_tmp, 2048))
            P.act(lambda e: e.activation(out=tmp[:, 0:NT], in_=tmp[:, 0:NT], func=AF.Ln, bias=EPS), r=pg(o_tmp, 2048), w=pg(o_tmp, 2048))
            P.act(lambda e: e.activation(out=A[:, 0:NT], in_=tmp[:, 0:NT], func=AF.Exp, scale=-0.5), r=pg(o_tmp, 2048), w=pg(o_A, 2048))

            def sub(c):
                P.dve(lambda e: e.tensor_tensor(out=src_of(c), in0=src_of(c), in1=PS[6][:, 0:NT], op=ALU.subtract), r=src_keys(c) + [psk(6)], w=src_keys(c))
            sub(0); sub(1)
            for c in range(8):
                P.dve(lambda e, c=c: e.tensor_tensor(out=src_of(c), in0=src_of(c), in1=A[:, 0:NT], op=ALU.mult), r=src_keys(c) + pg(o_A, 2048), w=src_keys(c))
                if dstbf_of is not None:
                    P.act(lambda e, c=c: e.activation(out=dstbf_of(c), in_=src_of(c), func=AF.Identity, bias=bcol(c), scale=gcol(c)), r=src_keys(c), w=[('xbf', c)])
                if dstbf_of is None or not defer:
                    P.act(lambda e, c=c: e.activation(out=src_of(c), in_=src_of(c), func=AF.Identity, bias=bcol(c), scale=gcol(c)), r=src_keys(c), w=src_keys(c))
                if c + 2 < 8:
                    sub(c + 2)
            if dstbf_of is not None and defer:
                def fin_fp32():
                    for c in range(8):
                        P.act(lambda e, c=c: e.activation(out=src_of(c), in_=src_of(c), func=AF.Identity, bias=bcol(c), scale=gcol(c)), r=src_keys(c), w=src_keys(c))
                pending_fin.append(fin_fp32)

        pending_fin = []

        def flush_fin():
            while pending_fin:
                pending_fin.pop(0)()

        def stats_ew(NT, c, src_ap, src_keys):
            rb = rbuf[c % 2]; rs = rsq[c % 2]
            P.act(lambda e: e.copy(out=rb[:, 0:NT], in_=src_ap), r=src_keys, w=[('rb', c % 2)])
            P.act(lambda e: e.activation(out=rs[:, 0:NT], in_=src_ap, func=AF.Square), r=src_keys, w=[('rs', c % 2)])

        def stats_pe(NT, c):
            rb = rbuf[c % 2]; rs = rsq[c % 2]
            P.pe(lambda e: e.matmul(PS[6][:, 0:NT], lhsT=inv_b[:, :], rhs=rb[:, 0:NT], start=(c == 0), stop=(c == 7)), r=[('rb', c % 2), 'inv_b'], w=[psk(6)])
            P.pe(lambda e: e.matmul(PS[7][:, 0:NT], lhsT=inv_b[:, :], rhs=rs[:, 0:NT], start=(c == 0), stop=(c == 7)), r=[('rs', c % 2), 'inv_b'], w=[psk(7)])

        def proj_postnorm(NT, l, k, mm_for_chunk, need_bf=True, defer=True):
            flush_fin()
            pend = None
            for m in range(8):
                bank = mmbank()
                mm_for_chunk(m, bank)
                if pend is not None:
                    stats_pe(NT, pend)
                P.dve(lambda e, m=m, bank=bank: e.scalar_tensor_tensor(out=xT[:, m, 0:NT], in0=xT[:, m, 0:NT], scalar=ALPHA, in1=PS[bank][:, 0:NT],
                                                                       op0=ALU.mult, op1=ALU.add), r=[psk(bank), ('xT', m)], w=[('xT', m)])
                stats_ew(NT, m, xT[:, m, 0:NT], [('xT', m)])
                pend = m
            stats_pe(NT, pend)
            ln_apply(NT, lambda c: xT[:, c, 0:NT], (lambda c: xbf[:, c, 0:NT]) if need_bf else None, lambda c: lng(l, k, c), lambda c: lnb(l, k, c), lambda c: [('xT', c)], defer=defer)

        def simple_proj(NT, name, nk, src_of, src_keys, l, k):
            cw = 2048 // nk
            per_blk = cw // 128
            st = {}

            def mm(m, bank):
                if m % per_blk == 0:
                    st['blk'] = wblock(name, m * 128, cw)
                wvw, wk = st['blk']
                mo = (m % per_blk) * 128
                for kc in range(nk):
                    P.pe(lambda e, kc=kc: e.matmul(PS[bank][:, 0:NT], lhsT=wvw[:, kc, mo:mo + 128], rhs=src_of(kc), start=(kc == 0), stop=(kc == nk - 1)),
                         r=[wk] + src_keys(kc), w=[psk(bank)])
            proj_postnorm(NT, l, k, mm, defer=(k != 0))

        xdone = set()

        def xload(tj, sample, sub):
            if (tj, sample, sub) in xdone:
                return
            xdone.add((tj, sample, sub))
            sw_ = 64 if sample else 128
            t0_ = 0 if sample else tj * 512
            src_ = xs if sample else xp
            xi = xin[sub % 2]
            P.dma('sp', f'xi{sub % 2}', lambda e: e.dma_start(out=xi[0:sw_, :], in_=src_[t0_ + sub * 128: t0_ + sub * 128 + sw_, :]), w=[f'xin{sub % 2}'])

        def xprefetch(tj, sample):
            if sample:
                return
            if tj + 1 < nprompt:
                xload(tj + 1, False, 0); xload(tj + 1, False, 1)
            elif do_sample:
                xload(0, True, 0)

        def tile_pass(tj, sample):
            NT = 64 if sample else 512
            nsub = 1 if sample else 4
            sw = 64 if sample else 128
            tok0 = 0 if sample else tj * 512
            xsrc = xs if sample else xp
            blk0 = 8 if sample else tj * 4
            kcol0 = 1024 if sample else tok0
            lf_ = lf_s if sample else lf; ctok_ = ctok_s if sample else ctok; carry_ = carry_s if sample else carry
            ck = 's' if sample else 'p'
            for sub in range(nsub):
                xi = xin[sub % 2]; xk = f'xin{sub % 2}'
                xload(tj, sample, sub)
                for half in range(2):
                    bank = mmbank()
                    for q in range(4):
                        c = half * 4 + q
                        P.pe(lambda e, xi=xi, c=c, q=q, bank=bank: e.transpose(out=PS[bank][:, q * 128:q * 128 + sw], in_=xi[0:sw, c * 128:(c + 1) * 128], identity=ident[0:sw, 0:sw]),
                             r=[xk, 'ident'], w=[psk(bank)])
                    srcv = PS[bank][:, :].rearrange("p (a b) -> p a b", a=4)[:, :, 0:sw]
                    P.dve(lambda e, half=half, sub=sub, srcv=srcv: e.tensor_copy(out=xbf[:, half * 4:half * 4 + 4, sub * 128:sub * 128 + sw], in_=srcv),
                          r=[psk(bank)], w=[('xbf', half * 4 + q) for q in range(4)])
                    P.act(lambda e, half=half, sub=sub, srcv=srcv: e.copy(out=xT[:, half * 4:half * 4 + 4, sub * 128:sub * 128 + sw], in_=srcv),
                          r=[psk(bank)], w=[('xT', half * 4 + q) for q in range(4)])
            xbk = [('xbf', c) for c in range(8)]

            def mm_interleaved(accs):
                for kc in range(8):
                    for (bank, lfn, wk) in accs:
                        P.pe(lambda e, kc=kc, bank=bank, lfn=lfn: e.matmul(PS[bank][:, 0:NT], lhsT=lfn(kc), rhs=xbf[:, kc, 0:NT], start=(kc == 0), stop=(kc == 7)),
                             r=[wk, ('xbf', kc)], w=[psk(bank)])

            def emit_y():
                flush_fin()
                ydst = ys if sample else yp
                for sub in range(nsub):
                    for half in range(2):
                        bank = mmbank()
                        for q in range(4):
                            c = half * 4 + q
                            P.pe(lambda e, c=c, q=q, sub=sub, bank=bank: e.transpose(out=PS[bank][0:sw, q * 128:(q + 1) * 128], in_=xT[:, c, sub * 128:sub * 128 + sw], identity=ident[:]),
                                 r=[('xT', c), 'ident'], w=[psk(bank)])
                        stage_out(PS[bank][0:sw, :], sw, 512, ydst[tok0 + sub * 128: tok0 + sub * 128 + sw, half * 512:(half + 1) * 512], [psk(bank)])
            if dbg <= 1:
                emit_y(); return

            o_QT, o_hT, o_bg, o_cat, o_ca, o_RQ = 0, 8192, 16384, 24576, 32768, 8192
            QTz = wv(o_QT, (8, 512), BF16); hT = wv(o_hT, (4, 512), F32); bgT = wv(o_bg, (4, 512), F32)
            cat = wv(o_cat, (8, 512), BF16)
            ca = wv(o_ca, (512,), F32)
            ca4 = wv(o_hT, (4, 512), F32)
            if sample:
                kst = wv(0, (8, 512), F32)
                P.dma('pool', 'c_kst', lambda e: e.dma_start(out=kst, in_=cfk.rearrange("(a p) d -> p a d", p=128)), w=pg(0, 16384))
                for b in range(8):
                    for par in range(2):
                        P.dma('pool', f'c_vc{b}_{par}', lambda e, b=b, par=par: e.dma_start(out=Vs[:, b, :, par * 128:par * 128 + 64],
                              in_=cfv[b * 128:(b + 1) * 128, :].rearrange("p (c e d) -> p c e d", c=4, e=2)[:, :, par, :]), w=[('V', b, par)])
                P.dma('pool', 'c_lf', lambda e: e.dma_start(out=lf_[:, 0:8, :], in_=cfl.rearrange("(a p) d -> p a d", p=128)), w=[('lf' + ck, b) for b in range(8)])
                for cc in range(4):
                    P.dma('pool', 'c_pre', lambda e, cc=cc: e.dma_start(out=prebuf[:, cc, 0:2], in_=sconv[:, cc * 128:(cc + 1) * 128].rearrange("j p -> p j"), allow_slow_non_contiguous=True),
                          r=['prebuf'], w=['prebuf'])
                for b in range(8):
                    bank = mmbank()
                    for c in range(4):
                        P.pe(lambda e, b=b, c=c, bank=bank: e.transpose(out=PS[bank][:, c * 128:(c + 1) * 128], in_=kst[:, b, c * 128:(c + 1) * 128], identity=ident[:]),
                             r=pg(0, 16384) + ['ident'], w=[psk(bank)])
                    P.act(lambda e, b=b, bank=bank: e.copy(out=KT[:, :, b * 128:(b + 1) * 128], in_=PS[bank][:, :].rearrange("p (a b) -> p a b", a=4)),
                          r=[psk(bank)], w=[('KT', c, b // 4) for c in range(4)])
                mks = wv(16384, (2, 512), F32)
                for l in range(2):
                    P.dma('pool', 'c_mks', lambda e, l=l: e.dma_start(out=mks, in_=cmk[l].rearrange("(a p) d -> p a d", p=128)), w=pg(16384, 4096))
                    P.dma('pool', f'c_mv{l}', lambda e, l=l: e.dma_start(out=memV[l][:, :, :], in_=cmv[l].rearrange("(a p) d -> p a d", p=128)), w=[('memV', l)])
                    for mb in range(2):
                        bank = mmbank()
                        for h in range(4):
                            P.pe(lambda e, mb=mb, h=h, bank=bank: e.transpose(out=PS[bank][:, h * 128:(h + 1) * 128], in_=mks[:, mb, h * 128:(h + 1) * 128], identity=ident[:]),
                                 r=pg(16384, 4096) + ['ident'], w=[psk(bank)])
                        P.act(lambda e, mb=mb, bank=bank, l=l: e.copy(out=memKT[l][:, :, mb * 128:(mb + 1) * 128], in_=PS[bank][:, :].rearrange("p (a b) -> p a b", a=4)),
                              r=[psk(bank)], w=[('memKT', l)])
            wvw, wk = wblock('win0', 1536, 8)
            for sub in range(nsub):
                blk = blk0 + sub
                bank = mmbank()
                for kc in range(8):
                    P.pe(lambda e, wvw=wvw, kc=kc, sub=sub, bank=bank: e.matmul(PS[bank][0:sw, 0:8], lhsT=xbf[:, kc, sub * 128:sub * 128 + sw], rhs=wvw[:, kc, :],
                                                                                 start=(kc == 0), stop=(kc == 7)), r=[wk, ('xbf', kc)], w=[psk(bank)])
                t = lft[:, sub % 2, :]; tk = ('lft', sub % 2)
                P.dve(lambda e, t=t, bank=bank: e.tensor_tensor(out=t[0:sw, :], in0=PS[bank][0:sw, 0:8], in1=bfb[0:sw, :], op=ALU.add), r=[psk(bank), 'bfb'], w=[tk])
                P.act(lambda e, t=t: e.activation(out=t[0:sw, :], in_=t[0:sw, :], func=AF.Exp, scale=-1.0), r=[tk], w=[tk])
                P.act(lambda e, t=t: e.activation(out=t[0:sw, :], in_=t[0:sw, :], func=AF.Ln, bias=1.0), r=[tk], w=[tk])
                P.dve(lambda e, t=t, blk=blk: e.tensor_scalar(out=lf_[0:sw, blk, :], in0=t[0:sw, :], scalar1=-1.0, scalar2=None, op0=ALU.mult), r=[tk], w=[('lf' + ck, blk)])
                dst = (fls if sample else flp)[tok0 + sub * 128: tok0 + sub * 128 + sw, :]
                P.dma('act', 'solf', lambda e, blk=blk, dst=dst: e.dma_start(out=dst, in_=lf_[0:sw, blk, :]), r=[('lf' + ck, blk)])
            P.dve(lambda e: e.memset(QTz[:, :, :], 0.0), w=pg(o_QT, 8192))
            for part, base in (('q', 0), ('k', 512)):
                for j in range(2):
                    wvw, wk = wblock('win0', base + j * 256, 256)
                    for mm_ in range(2):
                        c = j * 2 + mm_
                        bank = mmbank()
                        for kc in range(8):
                            P.pe(lambda e, wvw=wvw, kc=kc, mm_=mm_, bank=bank: e.matmul(PS[bank][:, 0:NT], lhsT=wvw[:, kc, mm_ * 128:(mm_ + 1) * 128], rhs=xbf[:, kc, 0:NT],
                                                                                         start=(kc == 0), stop=(kc == 7)), r=[wk, ('xbf', kc)], w=[psk(bank)])
                        if part == 'q':
                            P.act(lambda e, c=c, bank=bank: e.copy(out=QTz[0:64, 2 * c, 0:NT], in_=PS[bank][0:64, 0:NT]), r=[psk(bank)], w=pg(o_QT + 2 * c * 1024, 1024))
                            P.act(lambda e, c=c, bank=bank: e.copy(out=QTz[64:128, 2 * c + 1, 0:NT], in_=PS[bank][64:128, 0:NT]), r=[psk(bank)], w=pg(o_QT + (2 * c + 1) * 1024, 1024))
                        else:
                            P.act(lambda e, c=c, bank=bank: e.copy(out=KT[:, c, kcol0:kcol0 + NT], in_=PS[bank][:, 0:NT]), r=[psk(bank)],
                                  w=[('KT', c, kcol0 // 512)])
                    if part == 'k':
                        for sub in range(nsub):
                            bank = mmbank()
                            for kc in range(8):
                                P.pe(lambda e, wvw=wvw, kc=kc, sub=sub, bank=bank: e.matmul(PS[bank][0:sw, 0:256], lhsT=xbf[:, kc, sub * 128:sub * 128 + sw], rhs=wvw[:, kc, :],
                                                                                             start=(kc == 0), stop=(kc == 7)), r=[wk, ('xbf', kc)], w=[psk(bank)])
                            dst = (fks if sample else fkp)[tok0 + sub * 128: tok0 + sub * 128 + sw, j * 256:(j + 1) * 256]
                            stage_out(PS[bank][0:sw, 0:256], sw, 256, dst, [psk(bank)])
            vb0, vb1 = (0, 9) if sample else (blk0, blk0 + 4)
            P.dve(lambda e: e.memset(Vs[:, vb0:vb1, :, 64:128].rearrange("p a c d -> p (a c) d"), 1.0), w=[('Vones', b) for b in range(vb0, vb1)])
            for j in range(2):
                wvw, wk = wblock('win0', 1024 + j * 256, 256)
                for sub in range(nsub):
                    bank = mmbank()
                    for kc in range(8):
                        P.pe(lambda e, wvw=wvw, kc=kc, sub=sub, bank=bank: e.matmul(PS[bank][0:sw, 0:256], lhsT=xbf[:, kc, sub * 128:sub * 128 + sw], rhs=wvw[:, kc, :],
                                                                                     start=(kc == 0), stop=(kc == 7)), r=[wk, ('xbf', kc)], w=[psk(bank)])
                    for par in range(2):
                        P.dve(lambda e, sub=sub, j=j, bank=bank, par=par: e.tensor_copy(out=Vs[0:sw, blk0 + sub, 2 * j:2 * j + 2, par * 128:par * 128 + 64],
                                                                                      in_=PS[bank][0:sw, 0:256].rearrange("p (c e d) -> p c e d", c=2, e=2)[:, :, par, :]),
                              r=[psk(bank)], w=[('V', blk0 + sub, par)])
                    dst = (fvs if sample else fvp)[tok0 + sub * 128: tok0 + sub * 128 + sw, j * 256:(j + 1) * 256]
                    stage_out(PS[bank][0:sw, 0:256], sw, 256, dst, [psk(bank)])
            cblocks = list(range(0, 9)) if sample else [blk0 + s for s in range(nsub)]
            for blk in cblocks:
                w_ = 64 if (sample and blk == 8) else 128
                bank = mmbank()
                P.pe(lambda e, blk=blk, w_=w_, bank=bank: e.matmul(PS[bank][0:w_, 0:8], lhsT=tri[0:w_, 0:w_], rhs=lf_[0:w_, blk, :], start=True, stop=True),
                     r=[('lf' + ck, blk), 'tri'], w=[psk(bank)])
                P.pe(lambda e, blk=blk, w_=w_, bank=bank: e.matmul(PS[bank][:, 8:16], lhsT=onesf[0:w_, :], rhs=lf_[0:w_, blk, :], start=True, stop=True),
                     r=[('lf' + ck, blk), 'onesf'], w=[psk(bank)])
                P.dve(lambda e, blk=blk, w_=w_, bank=bank: e.tensor_tensor(out=ctok_[0:w_, blk, :], in0=PS[bank][0:w_, 0:8], in1=carry_[0:w_, blk, :], op=ALU.add),
                      r=[psk(bank), ('carry' + ck, blk)], w=[('ctok' + ck, blk)])
                P.dve(lambda e, blk=blk, bank=bank: e.tensor_tensor(out=carry_[:, blk + 1, :], in0=PS[bank][:, 8:16], in1=carry_[:, blk, :], op=ALU.add),
                      r=[psk(bank), ('carry' + ck, blk)], w=[('carry' + ck, blk + 1)])
            if dbg <= 2:
                emit_y(); return
            for j in range(6):
                wvw, wk = wblock('win0', 1544 + j * 256, 256)
                for mm_ in range(2):
                    cc = (j % 2) * 2 + mm_
                    bank = mmbank()
                    for kc in range(8):
                        P.pe(lambda e, wvw=wvw, kc=kc, mm_=mm_, bank=bank: e.matmul(PS[bank][:, 0:NT], lhsT=wvw[:, kc, mm_ * 128:(mm_ + 1) * 128], rhs=xbf[:, kc, 0:NT],
                                                                                     start=(kc == 0), stop=(kc == 7)), r=[wk, ('xbf', kc)], w=[psk(bank)])
                    if j < 2:
                        P.act(lambda e, cc=cc, bank=bank: e.copy(out=hT[:, cc, 0:NT], in_=PS[bank][:, 0:NT]), r=[psk(bank)], w=pg(o_hT + cc * 2048, 2048))
                    elif j < 4:
                        P.act(lambda e, cc=cc, bank=bank: e.copy(out=bgT[:, cc, 0:NT], in_=PS[bank][:, 0:NT]), r=[psk(bank)], w=pg(o_bg + cc * 2048, 2048))
                    else:
                        P.dve(lambda e, cc=cc, bank=bank: e.tensor_tensor(out=prebuf[:, cc, 2:2 + NT], in0=PS[bank][:, 0:NT], in1=hT[:, cc, 0:NT], op=ALU.mult),
                              r=[psk(bank)] + pg(o_hT + cc * 2048, 2048), w=['prebuf'])
                        P.act(lambda e, cc=cc: e.activation(out=ca4[:, cc, 0:NT], in_=prebuf[:, cc, 0:NT], func=AF.Copy, scale=cwcol(0, cc)),
                              r=['prebuf', 'colsB'], w=pg(o_hT + cc * 2048, 2048))
            if dbg <= 3:
                emit_y(); return
            nblk = blk0 + nsub
            endidx = blk0 + nsub
            for b in range(nblk):
                kw = 64 if (sample and b == 8) else 128
                P.dve(lambda e, b=b, kw=kw: e.tensor_tensor(out=biasb[0:kw, b, :], in0=carry_[0:kw, endidx, :], in1=ctok_[0:kw, b, :], op=ALU.subtract),
                      r=[('carry' + ck, endidx), ('ctok' + ck, b)], w=[('bias', b)])
            if not sample:
                for i in range(4):
                    P.dve(lambda e, i=i: e.tensor_tensor(out=T2[0:1, i, :], in0=carry_[0:1, blk0 + i + 1, :], in1=carry_[0:1, endidx, :], op=ALU.subtract),
                          r=[('carry' + ck, blk0 + i + 1), ('carry' + ck, endidx)], w=['T2'])
                for h in range(8):
                    P.dve(lambda e, h=h: e.tensor_copy(out=xbf[0:1, h, :].rearrange("p (i j) -> p i j", i=4), in_=T2[0:1, :, h:h + 1].to_broadcast([1, 4, 128])),
                          r=['T2'], w=[('xbf', h)])
            for cc in range(4):
                kc_ = pg(o_hT + cc * 2048, 2048)
                P.dve(lambda e, cc=cc: e.scalar_tensor_tensor(out=ca4[:, cc, 0:NT], in0=prebuf[:, cc, 1:1 + NT], scalar=cwcol(1, cc), in1=ca4[:, cc, 0:NT], op0=ALU.mult, op1=ALU.add),
                      r=['prebuf', 'colsB'] + kc_, w=kc_)
                P.dve(lambda e, cc=cc: e.scalar_tensor_tensor(out=ca4[:, cc, 0:NT], in0=prebuf[:, cc, 2:2 + NT], scalar=cwcol(2, cc), in1=ca4[:, cc, 0:NT], op0=ALU.mult, op1=ALU.add),
                      r=['prebuf', 'colsB'] + kc_, w=kc_)
                P.dve(lambda e, cc=cc: e.tensor_tensor(out=cat[:, 4 + cc, 0:NT], in0=ca4[:, cc, 0:NT], in1=bgT[:, cc, 0:NT], op=ALU.mult),
                      r=kc_ + pg(o_bg + cc * 2048, 2048), w=pg(o_cat + (4 + cc) * 1024, 1024))
            last_prompt = (not sample) and tj == nprompt - 1
            if sample or last_prompt:
                dsto = (css if sample else csp)
                for cc in range(4):
                    P.dma('act', 'socs', lambda e, cc=cc: e.dma_start(out=dsto[:, cc * 128:(cc + 1) * 128].rearrange("j p -> p j"), in_=prebuf[:, cc, NT:NT + 2], allow_slow_non_contiguous=True), r=['prebuf'])
            else:
                P.dve(lambda e: e.tensor_copy(out=prebuf[:, :, 0:2], in_=prebuf[:, :, NT:NT + 2]), r=['prebuf'], w=['prebuf'])
            steps = []
            for c in range(4):
                for b in range(nblk):
                    for e_ in range(2):
                        steps.append((c, e_, b))
            LOOK = 3
            nst = len(steps)
            rcF = wv(o_ca, (512,), F32)

            def geom(b):
                kw = 64 if (sample and b == 8) else 128
                d = b - blk0
                qlo = 128 * d if d > 0 else 0
                return kw, d, qlo

            def emit_qk(n):
                c, e_, b = steps[n]
                kw, d, qlo = geom(b)
                sl = n % 4
                sbk = (0, 1, 6, 7)[sl]
                h = 2 * c + e_
                Sap = PS[sbk][0:kw, qlo:NT]
                aug = (not sample) and qlo < 384
                P.pe(lambda e: e.matmul(Sap, lhsT=KT[:, c, b * 128:b * 128 + kw], rhs=QTz[:, h, qlo:NT], start=True, stop=(not aug)),
                     r=[('KT', c, (b * 128) // 512)] + pg(o_QT + h * 1024, 1024), w=[psk(sbk)])
                if aug:
                    P.pe(lambda e: e.matmul(PS[sbk][0:kw, qlo:384], lhsT=eight_b[:, 0:kw], rhs=xbf[:, h, qlo:384], start=False, stop=True),
                         r=['eight_b', ('xbf', h)], w=[psk(sbk)])
                P.act(lambda e: e.activation(out=Pb[sl][0:kw, qlo:NT], in_=Sap, func=AF.Exp, bias=biasb[0:kw, b, h:h + 1], scale=0.125),
                      r=[psk(sbk), ('bias', b)], w=[('P', sl)])
                if d >= 0:
                    qwm = min(128, NT - qlo)
                    P.dve(lambda e: e.tensor_tensor(out=Pb[sl][0:kw, qlo:qlo + qwm], in0=Pb[sl][0:kw, qlo:qlo + qwm], in1=mask_b[0:kw, 0:qwm], op=ALU.mult),
                          r=[('P', sl), 'mask_b'], w=[('P', sl)])

            def accbank(c, e_):
                return 2 + (c % 2) * 2 + e_

            def emit_pv(n):
                c, e_, b = steps[n]
                kw, d, qlo = geom(b)
                sl = n % 4
                h = 2 * c + e_
                bk = accbank(c, e_)
                lw = Vs[0:kw, b, c, e_ * 64:e_ * 64 + 128]
                P.pe(lambda e: e.matmul(PS[bk][:, qlo:NT], lhsT=lw, rhs=Pb[sl][0:kw, qlo:NT], start=(b == 0), stop=(b == nblk - 1)),
                     r=[('V', b, e_), ('P', sl), ('Vones', b)], w=[psk(bk)])
                if e_ == 1 and b == nblk - 1:
                    for ee in range(2):
                        bk2 = accbank(c, ee)
                        po, pd = (0, 64) if ee == 0 else (64, 0)
                        if False:
                            P.act(lambda e, bk2=bk2, pd=pd: e.activation(out=rcF[pd:pd + 64, 0:NT], in_=PS[bk2][pd:pd + 64, 0:NT], func=AF.Ln), r=[psk(bk2)], w=pg(o_ca, 2048))
                            P.act(lambda e, pd=pd: e.activation(out=rcF[pd:pd + 64, 0:NT], in_=rcF[pd:pd + 64, 0:NT], func=AF.Exp, scale=-1.0), r=pg(o_ca, 2048), w=pg(o_ca, 2048))
                        else:
                            P.dve(lambda e, bk2=bk2, pd=pd: e.reciprocal(out=rcF[pd:pd + 64, 0:NT], in_=PS[bk2][pd:pd + 64, 0:NT]), r=[psk(bk2)], w=pg(o_ca, 2048))
                        P.dve(lambda e, bk2=bk2, po=po, pd=pd: e.tensor_tensor(out=cat[po:po + 64, c, 0:NT], in0=PS[bk2][po:po + 64, 0:NT], in1=rcF[pd:pd + 64, 0:NT], op=ALU.mult),
                              r=[psk(bk2)] + pg(o_ca, 2048), w=pg(o_cat + c * 1024, 1024))

            for n in range(nst + LOOK):
                if n < nst:
                    emit_qk(n)
                if n - LOOK >= 0:
                    emit_pv(n - LOOK)
            simple_proj(NT, 'wout0', 8, lambda kc: cat[:, kc, 0:NT], lambda kc: pg(o_cat + kc * 1024, 1024), 0, 0)

            def mem_attn(l):
                o_QM, o_PM, o_OM, o_rc = 0, 4096, 8192, 12288
                QM = wv(o_QM, (4, 512), BF16); PM = wv(o_PM, (4, 512), BF16); OM = wv(o_OM, (4, 512), BF16)
                rcm = [wv(o_rc + i * 2048, (512,), F32) for i in range(2)]
                qblks = [wblock(f'mq{l}', j * 256, 256) for j in range(2)]
                qbanks = [mmbank() for _ in range(4)]
                mm_interleaved([(qbanks[h], (lambda kc, h=h: qblks[h // 2][0][:, kc, (h % 2) * 128:(h % 2) * 128 + 128]), qblks[h // 2][1]) for h in range(4)])
                for h in range(4):
                    P.dve(lambda e, h=h: e.tensor_copy(out=QM[:, h, 0:NT], in_=PS[qbanks[h]][:, 0:NT]), r=[psk(qbanks[h])], w=pg(o_QM + h * 1024, 1024))
                flush_fin()
                def fin(h):
                    s0 = (h % 2) * 4
                    rc = rcm[h % 2]; rk = pg(o_rc + (h % 2) * 2048, 2048)
                    P.act(lambda e: e.activation(out=rc[:, 0:NT], in_=PS[s0 + 3][:, 0:NT], func=AF.Ln), r=[psk(s0 + 3)], w=rk)
                    P.act(lambda e: e.activation(out=rc[:, 0:NT], in_=rc[:, 0:NT], func=AF.Exp, scale=-1.0), r=rk, w=rk)
                    P.dve(lambda e: e.tensor_tensor(out=OM[:, h, 0:NT], in0=PS[s0 + 2][:, 0:NT], in1=rc[:, 0:NT], op=ALU.mult),
                          r=[psk(s0 + 2)] + rk, w=pg(o_OM + h * 1024, 1024))

                def scores(h):
                    s0 = (h % 2) * 4
                    for mb in range(2):
                        P.pe(lambda e, mb=mb: e.matmul(PS[s0 + mb][:, 0:NT], lhsT=memKT[l][:, h, mb * 128:(mb + 1) * 128], rhs=QM[:, h, 0:NT], start=True, stop=True),
                             r=[('memKT', l)] + pg(o_QM + h * 1024, 1024), w=[psk(s0 + mb)])
                        P.act(lambda e, mb=mb: e.activation(out=PM[:, (h % 2) * 2 + mb, 0:NT], in_=PS[s0 + mb][:, 0:NT], func=AF.Exp, scale=float(128 ** -0.5)),
                              r=[psk(s0 + mb)], w=pg(o_PM + ((h % 2) * 2 + mb) * 1024, 1024))

                def pv(h):
                    s0 = (h % 2) * 4
                    for mb in range(2):
                        pk = pg(o_PM + ((h % 2) * 2 + mb) * 1024, 1024)
                        P.pe(lambda e, mb=mb: e.matmul(PS[s0 + 2][:, 0:NT], lhsT=memV[l][:, mb, h * 128:(h + 1) * 128], rhs=PM[:, (h % 2) * 2 + mb, 0:NT],
                                                       start=(mb == 0), stop=(mb == 1)), r=[('memV', l)] + pk, w=[psk(s0 + 2)])
                        P.pe(lambda e, mb=mb: e.matmul(PS[s0 + 3][:, 0:NT], lhsT=ones_b[:, :], rhs=PM[:, (h % 2) * 2 + mb, 0:NT],
                                                       start=(mb == 0), stop=(mb == 1)), r=['ones_b'] + pk, w=[psk(s0 + 3)])

                scores(0)
                scores(1)
                for h in range(4):
                    pv(h)
                    if h >= 1:
                        fin(h - 1)
                    if h + 2 < 4:
                        scores(h + 2)
                fin(3)
                simple_proj(NT, f'mo{l}', 4, lambda kc: OM[:, kc, 0:NT], lambda kc: pg(o_OM + kc * 1024, 1024), l, 1)

            def ffn(l):
                o_HT, o_sg = 0, 22528
                HT = wv(o_HT, (22, 512), BF16)
                sg = [wv(o_sg + i * 2048, (512,), F32) for i in range(2)]
                for j in range(11):
                    gw, gk = wblock(f'fg{l}', j * 256, 256)
                    uw, uk = wblock(f'fu{l}', j * 256, 256)
                    fbanks = [(mmbank(), mmbank()) for _ in range(2)]
                    if j == 0:
                        accs = []
                        for mm_ in range(2):
                            accs.append((fbanks[mm_][0], (lambda kc, mm_=mm_, gw=gw: gw[:, kc, mm_ * 128:(mm_ + 1) * 128]), gk))
                            accs.append((fbanks[mm_][1], (lambda kc, mm_=mm_, uw=uw: uw[:, kc, mm_ * 128:(mm_ + 1) * 128]), uk))
                        mm_interleaved(accs)
                    for mm_ in range(2):
                        f = j * 2 + mm_
                        bg_, bu_ = fbanks[mm_]
                        if j > 0:
                            for kc in range(8):
                                P.pe(lambda e, kc=kc, mm_=mm_, bg_=bg_, gw=gw: e.matmul(PS[bg_][:, 0:NT], lhsT=gw[:, kc, mm_ * 128:(mm_ + 1) * 128], rhs=xbf[:, kc, 0:NT],
                                                                                  start=(kc == 0), stop=(kc == 7)), r=[gk, ('xbf', kc)], w=[psk(bg_)])
                            for kc in range(8):
                                P.pe(lambda e, kc=kc, mm_=mm_, bu_=bu_, uw=uw: e.matmul(PS[bu_][:, 0:NT], lhsT=uw[:, kc, mm_ * 128:(mm_ + 1) * 128], rhs=xbf[:, kc, 0:NT],
                                                                                  start=(kc == 0), stop=(kc == 7)), r=[uk, ('xbf', kc)], w=[psk(bu_)])
                        s = sg[f % 2]; sk = pg(o_sg + (f % 2) * 2048, 2048)
                        P.act(lambda e, s=s, bg_=bg_: e.activation(out=s[:, 0:NT], in_=PS[bg_][:, 0:NT], func=AF.Silu), r=[psk(bg_)], w=sk)
                        P.dve(lambda e, s=s, bu_=bu_, f=f: e.tensor_tensor(out=HT[:, f, 0:NT], in0=PS[bu_][:, 0:NT], in1=s[:, 0:NT], op=ALU.mult),
                              r=[psk(bu_)] + sk, w=pg(o_HT + f * 1024, 1024))
                    flush_fin()
                P.act(lambda e: e.activation(out=dmy[:, 0:1], in_=dmy[:, 1:2], func=AF.Ln), r=['dmy'], w=['dmy'])

                def mm(m, bank):
                    for kh in range(2):
                        wvw, wk = wblock(f'fd{l}', m * 128, 128, k0=kh * 11, nk=11)
                        for kk in range(11):
                            kc = kh * 11 + kk
                            P.pe(lambda e, wvw=wvw, kk=kk, kc=kc: e.matmul(PS[bank][:, 0:NT], lhsT=wvw[:, kk, :], rhs=HT[:, kc, 0:NT], start=(kc == 0), stop=(kc == 21)),
                                 r=[wk] + pg(o_HT + kc * 1024, 1024), w=[psk(bank)])
                proj_postnorm(NT, l, 2, mm, need_bf=(l == 0))

            if dbg <= 5:
                emit_y(); return
            mem_attn(0)
            if dbg <= 6:
                emit_y(); return
            ffn(0)
            if dbg <= 7:
                emit_y(); return

            o_UT, o_VT, o_VN, o_GT = 0, 8192, 24576, 12288
            UT = wv(o_UT, (8, 512), BF16); VT = wv(o_VT, (8, 512), F32)
            VN = wv(o_VN, (4, 1024), BF16); GT = wv(o_GT, (8, 512), BF16)
            ublks = [wblock('win1', j * 256, 256) for j in range(2)]
            ubanks = [mmbank() for _ in range(4)]
            mm_interleaved([(ubanks[h], (lambda kc, h=h: ublks[h // 2][0][:, kc, (h % 2) * 128:(h % 2) * 128 + 128]), ublks[h // 2][1]) for h in range(4)])
            for j in range(8):
                if j >= 2:
                    wvw, wk = wblock('win1', j * 256, 256)
                for mm_ in range(2):
                    cch = j * 2 + mm_
                    if j < 2:
                        bank = ubanks[cch]
                    else:
                        bank = mmbank()
                        for kc in range(8):
                            P.pe(lambda e, wvw=wvw, kc=kc, mm_=mm_, bank=bank: e.matmul(PS[bank][:, 0:NT], lhsT=wvw[:, kc, mm_ * 128:(mm_ + 1) * 128], rhs=xbf[:, kc, 0:NT],
                                                                                         start=(kc == 0), stop=(kc == 7)), r=[wk, ('xbf', kc)], w=[psk(bank)])
                    if cch < 8:
                        P.act(lambda e, cch=cch, bank=bank: e.activation(out=UT[:, cch, 0:NT], in_=PS[bank][:, 0:NT], func=AF.Gelu, bias=bincol(cch)),
                              r=[psk(bank), 'colsB'], w=pg(o_UT + cch * 1024, 1024))
                        if cch == 3:
                            flush_fin()
                    else:
                        c = cch - 8
                        P.act(lambda e, c=c, cch=cch, bank=bank: e.activation(out=VT[:, c, 0:NT], in_=PS[bank][:, 0:NT], func=AF.Gelu, bias=bincol(cch)),
                              r=[psk(bank), 'colsB'], w=pg(o_VT + c * 2048, 2048))
                        stats_ew(NT, c, VT[:, c, 0:NT], pg(o_VT + c * 2048, 2048))
                        if c > 0:
                            stats_pe(NT, c - 1)
            P.act(lambda e: e.activation(out=dmy[:, 0:1], in_=dmy[:, 1:2], func=AF.Ln), r=['dmy'], w=['dmy'])
            stats_pe(NT, 7)
            ln_apply(NT, lambda c: VT[:, c, 0:NT], None, gngcol, gnbcol, lambda c: pg(o_VT + c * 2048, 2048))
            for sub in range(nsub):
                for half in range(2):
                    bank = mmbank()
                    for q in range(4):
                        c = half * 4 + q
                        P.pe(lambda e, c=c, q=q, sub=sub, bank=bank: e.transpose(out=PS[bank][0:sw, q * 128:(q + 1) * 128], in_=VT[:, c, sub * 128:sub * 128 + sw], identity=ident[:]),
                             r=pg(o_VT + c * 2048, 2048) + ['ident'], w=[psk(bank)])
                    P.act(lambda e, sub=sub, half=half, bank=bank: e.copy(out=VN[0:sw, sub, half * 512:(half + 1) * 512], in_=PS[bank][0:sw, :]),
                          r=[psk(bank)], w=pg(o_VN + sub * 2048, 2048))
                    if sample:
                        stage_out(PS[bank][0:sw, :], sw, 512, gvs[0:64, half * 512:(half + 1) * 512], [psk(bank)])
            tg = [wv(o_VT + i * 2048, (4, 128), F32) for i in range(2)]
            for sub in range(nsub):
                for half in range(2):
                    bank = mmbank()
                    for q in range(4):
                        g = half * 4 + q
                        P.pe(lambda e, g=g, q=q, sub=sub, bank=bank: e.matmul(PS[bank][:, q * 128:q * 128 + sw], lhsT=VN[0:sw, sub, g * 128:(g + 1) * 128], rhs=wsT[0:sw, g, 0:sw],
                                                                               start=True, stop=True), r=pg(o_VN + sub * 2048, 2048) + ['wsT'], w=[psk(bank)])
                    t = tg[half]; tk = pg(o_VT + half * 2048, 2048)
                    srcv = PS[bank][:, :].rearrange("p (a b) -> p a b", a=4)[:, :, 0:sw]
                    P.dve(lambda e, t=t, srcv=srcv, half=half: e.tensor_tensor(out=t[:, :, 0:sw], in0=srcv, in1=bsb[:, half * 4:half * 4 + 4, 0:sw], op=ALU.add),
                          r=[psk(bank), 'bsb'] + [k for c in range(8) for k in pg(o_VT + c * 2048, 2048)], w=tk)
                    P.dve(lambda e, t=t, sub=sub, half=half: e.tensor_tensor(out=GT[:, half * 4:half * 4 + 4, sub * 128:sub * 128 + sw], in0=t[:, :, 0:sw],
                                                                             in1=UT[:, half * 4:half * 4 + 4, sub * 128:sub * 128 + sw], op=ALU.mult),
                          r=tk + pg(o_UT + half * 4096, 4096), w=pg(o_GT + half * 4096, 4096))
            simple_proj(NT, 'wout1', 8, lambda kc: GT[:, kc, 0:NT], lambda kc: pg(o_GT + kc * 1024, 1024), 1, 0)
            if dbg <= 8:
                emit_y(); return
            mem_attn(1)
            xprefetch(tj, sample)
            ffn(1)
            emit_y()

        for tj in range(nprompt):
            tile_pass(tj, False)
            if tj == 0:
                P.fence([('wstg', g) for g in range(NSTG)], [('V', b, par) for b in range(4, 32) for par in range(2)] + [('Vones', b) for b in range(4, 32)])
        if do_sample:
            tile_pass(0, True)
        P.finalize(out_dma_sems=[k for k in ['so0', 'so1', 'solf', 'socs'] if k in P.dma_counts])
        P.emit()
    return nc, P


def make_in_maps(inp):
    f = lambda a: np.ascontiguousarray(a, dtype=np.float32)
    shared = {
        'win0': f(inp['w_in_even'][0]), 'wout0': f(inp['w_out_even'][0]),
        'win1': f(inp['w_in_odd'][0]), 'wout1': f(inp['w_out_odd'][0]),
        'b_forget': f(inp['b_forget']).reshape(1, 8), 'conv_w': f(inp['conv_w'][0]).reshape(12, 128),
        'b_in_odd': f(inp['b_in_odd'][0]).reshape(16, 128), 'gng': f(inp['gmlp_norm_g'][0]).reshape(8, 128),
        'gnb': f(inp['gmlp_norm_b'][0]).reshape(8, 128), 'gws': f(inp['gmlp_w_s'][0]),
        'gbs': f(inp['gmlp_b_s'][0]).reshape(1, 1024),
        'ln_g': f(inp['ln_g']).reshape(48, 128), 'ln_b': f(inp['ln_b']).reshape(48, 128),
    }
    for l in range(2):
        shared[f'mq{l}'] = f(inp['mem_w_q'][l]); shared[f'mk{l}'] = f(inp['mem_w_k'][l])
        shared[f'mv{l}'] = f(inp['mem_w_v'][l]); shared[f'mo{l}'] = f(inp['mem_w_o'][l])
        shared[f'fg{l}'] = f(inp['ffn_w_gate'][l]); shared[f'fu{l}'] = f(inp['ffn_w_up'][l])
        shared[f'fd{l}'] = f(inp['ffn_w_down'][l])
    maps = []
    for b in range(8):
        m = dict(shared)
        m['xp'] = f(inp['x_prompt'][b]); m['xs'] = f(inp['x_sample'][b])
        m['cfk'] = f(inp['cache_fox_k'][0, b]).reshape(1024, 512)
        m['cfv'] = f(inp['cache_fox_v'][0, b]).reshape(1024, 512)
        m['cfl'] = f(inp['cache_fox_logf'][0, b])
        m['sconv'] = f(inp['state_conv'][0, b])
        m['cmk'] = f(inp['cache_mem_k'][:, b]).reshape(2, 256, 512)
        m['cmv'] = f(inp['cache_mem_v'][:, b]).reshape(2, 256, 512)
        m['memp'] = f(inp['mem_prompt'][b])
        maps.append(m)
    return maps


_CACHE = {}


def kernel(**inputs):
    inp = {k: np.asarray(v) for k, v in inputs.items()}
    if 'nc' not in _CACHE:
        _CACHE['nc'] = build_program()[0]
    nc = _CACHE['nc']
    maps = make_in_maps(inp)
    res = run_bass_kernel_spmd(nc, maps, core_ids=list(range(8)))
    R = res.results
    st = lambda name: np.stack([np.asarray(R[b][name], dtype=np.float32) for b in range(8)], axis=0)
    y_prompt = st('yp'); y_sample = st('ys')
    fox_k_prompt = st('fkp').reshape(1, 8, 4096, 8, 64)
    fox_v_prompt = st('fvp').reshape(1, 8, 4096, 8, 64)
    fox_logf_prompt = st('flp').reshape(1, 8, 4096, 8)
    conv_state_prompt = st('csp').reshape(1, 8, 2, 512)
    mem_k_prompt = np.transpose(st('mkp'), (1, 0, 2, 3)).reshape(2, 8, 256, 4, 128)
    mem_v_prompt = np.transpose(st('mvp'), (1, 0, 2, 3)).reshape(2, 8, 256, 4, 128)
    fox_k_sample = st('fks').reshape(1, 8, 64, 8, 64)
    fox_v_sample = st('fvs').reshape(1, 8, 64, 8, 64)
    fox_logf_sample = st('fls').reshape(1, 8, 64, 8)
    conv_state_sample = st('css').reshape(1, 8, 2, 512)
    gmlp_v_sample = st('gvs').reshape(1, 8, 64, 1024)
    return (y_prompt, y_sample, fox_k_prompt, fox_v_prompt, fox_logf_prompt, conv_state_prompt,
            mem_k_prompt, mem_v_prompt, fox_k_sample, fox_v_sample, fox_logf_sample, conv_state_sample,
            gmlp_v_sample)
```

```python
import contextlib
import numpy as np
import concourse.bass as bass
import concourse.mybir as mybir
from concourse.bass_utils import run_bass_kernel_spmd

F32 = mybir.dt.float32
BF16 = mybir.dt.bfloat16
AF = mybir.ActivationFunctionType
ALU = mybir.AluOpType

ENGS = ('pe', 'act', 'dve', 'pool', 'sp')

NPROMPT_TILES = 8
DO_SAMPLE = True
NSLOT = 4
ALPHA = float((2 * 2) ** 0.25)
EPS = 1e-5


class Op:
    __slots__ = ('eng', 'fn', 'reads', 'writes', 'dma', 'idx', 'eidx', 'waits', 'marked', 'dmaval', 'clock')


class Prog:
    def __init__(self, nc):
        self.nc = nc
        self.ops = []
        self.dma_counts = {}

    def op(self, eng, fn, reads=(), writes=(), dma=None):
        o = Op()
        o.eng = eng; o.fn = fn; o.reads = tuple(reads); o.writes = tuple(writes); o.dma = dma
        o.idx = len(self.ops); o.waits = []; o.marked = False; o.dmaval = None
        if dma is not None:
            self.dma_counts[dma] = self.dma_counts.get(dma, 0) + 16
            o.dmaval = self.dma_counts[dma]
        self.ops.append(o)
        return o

    def pe(self, fn, r=(), w=()): return self.op('pe', fn, r, w)
    def act(self, fn, r=(), w=()): return self.op('act', fn, r, w)
    def dve(self, fn, r=(), w=()): return self.op('dve', fn, r, w)
    def pool(self, fn, r=(), w=()): return self.op('pool', fn, r, w)
    def dma(self, q, sem, fn, r=(), w=()): return self.op(q, fn, r, w, dma=sem)

    def op_at(self, pos, eng, fn, reads=(), writes=(), dma=None):
        o = self.op(eng, fn, reads, writes, dma)
        self.ops.pop()
        self.ops.insert(pos, o)
        return o

    def fence(self, src_keys, dst_keys):
        self.ops.append(('fence', tuple(src_keys), tuple(dst_keys)))

    def _semkey(self, o): return ('d', o.dma) if o.dma is not None else ('e', o.eng)
    def _semval(self, o): return o.dmaval if o.dma is not None else o.eidx

    def finalize(self, out_dma_sems=()):
        raw = self.ops
        ops = [o for o in raw if isinstance(o, Op)]
        for n, o in enumerate(ops):
            o.idx = n
        ecount = {e: 0 for e in ENGS}
        for o in ops:
            if o.dma is None:
                ecount[o.eng] += 1
                o.eidx = ecount[o.eng]
            else:
                o.eidx = None
        writers = {}
        readers = {}
        eng_clock = {e: {} for e in ENGS}
        ps_last = {}
        for o in raw:
            if not isinstance(o, Op):
                _, src, dst = o
                ws = []; rs = []
                for k in src:
                    ws += writers.get(k, []); rs += readers.get(k, [])
                for k in dst:
                    writers[k] = writers.get(k, []) + ws
                    readers[k] = readers.get(k, []) + rs
                continue
            deps = []
            for k in o.reads:
                for i in writers.get(k, ()):
                    deps.append((i, 'raw'))
            for k in o.writes:
                for i in writers.get(k, ()):
                    deps.append((i, 'waw'))
                for i in readers.get(k, ()):
                    deps.append((i, 'war'))
            for k in set(o.reads) | set(o.writes):
                if isinstance(k, tuple) and k and k[0] == 'ps':
                    d = ps_last.setdefault(k, {})
                    for en, i in d.items():
                        if en != o.eng:
                            deps.append((i, 'x'))
                    d[o.eng] = o.idx
            clock = eng_clock[o.eng]
            need = {}
            for (i, kind) in deps:
                p = ops[i]
                if p is o:
                    continue
                if p.dma is None and p.eng == o.eng:
                    if o.eng in ('pe', 'sp'):
                        continue
                sk = self._semkey(p); sv = self._semval(p)
                if clock.get(sk, 0) >= sv:
                    continue
                if need.get(sk, (0, None))[0] < sv:
                    need[sk] = (sv, p)
            for sk, (sv, p) in need.items():
                if clock.get(sk, 0) >= sv:
                    continue
                o.waits.append(p)
                p.marked = True
                for k2, v2 in p.clock.items():
                    if clock.get(k2, 0) < v2:
                        clock[k2] = v2
                if clock.get(sk, 0) < sv:
                    clock[sk] = sv
            o.clock = dict(clock)
            o.clock[self._semkey(o)] = self._semval(o)
            if o.dma is None and o.eng == 'pe':
                clock[('e', 'pe')] = o.eidx
            for k in o.writes:
                writers[k] = [o.idx]
                readers[k] = []
            for k in o.reads:
                readers.setdefault(k, []).append(o.idx)
        self.inc_no = {}
        cnt = {e: 0 for e in ENGS}
        for o in ops:
            if o.dma is None and o.marked:
                cnt[o.eng] += 1
                self.inc_no[o.idx] = cnt[o.eng]
        self.out_dma_sems = tuple(out_dma_sems)
        self.ops = ops

    def emit(self):
        nc = self.nc
        with contextlib.ExitStack() as es:
            esem = {e: es.enter_context(nc.semaphore('s_' + e)) for e in ENGS}
            dsem = {k: es.enter_context(nc.semaphore('d_' + str(k))) for k in self.dma_counts}
            block = es.enter_context(nc.Block())
            per = {e: [o for o in self.ops if o.eng == e] for e in ENGS}

            def run(eng_name, eng):
                for o in per[eng_name]:
                    for p in o.waits:
                        if p.dma is not None:
                            eng.wait_ge(dsem[p.dma], p.dmaval)
                        else:
                            eng.wait_ge(esem[p.eng], self.inc_no[p.idx])
                    ins = o.fn(eng)
                    if o.dma is not None:
                        ins.then_inc(dsem[o.dma], 16)
                    elif o.marked:
                        ins.then_inc(esem[o.eng], 1)
                if eng_name == 'sp':
                    for k in self.out_dma_sems:
                        eng.wait_ge(dsem[k], self.dma_counts[k])

            @block.tensor
            def _(e): run('pe', e)

            @block.scalar
            def _(e): run('act', e)

            @block.vector
            def _(e): run('dve', e)

            @block.gpsimd
            def _(e): run('pool', e)

            @block.sync
            def _(e): run('sp', e)


WSPEC = {
    'win0': (1024, 3080), 'wout0': (1024, 1024), 'win1': (1024, 2048), 'wout1': (1024, 1024),
}
for _l in range(2):
    WSPEC[f'mq{_l}'] = (1024, 512); WSPEC[f'mk{_l}'] = (1024, 512); WSPEC[f'mv{_l}'] = (1024, 512)
    WSPEC[f'mo{_l}'] = (512, 1024)
    WSPEC[f'fg{_l}'] = (1024, 2816); WSPEC[f'fu{_l}'] = (1024, 2816); WSPEC[f'fd{_l}'] = (2816, 1024)

WORK_BYTES = 34816
PAGE = 1024


def build_program(nprompt=NPROMPT_TILES, do_sample=DO_SAMPLE, dbg=99):
    nc = bass.Bass("TRN2", target_bir_lowering=False)
    S = 512 * nprompt
    din = lambda name, shape: nc.dram_tensor(name, list(shape), F32, kind="ExternalInput").ap()
    dout = lambda name, shape: nc.dram_tensor(name, list(shape), F32, kind="ExternalOutput").ap()
    xp = din('xp', (4096, 1024)); xs = din('xs', (64, 1024))
    cfk = din('cfk', (1024, 512)); cfv = din('cfv', (1024, 512)); cfl = din('cfl', (1024, 8))
    sconv = din('sconv', (2, 512))
    cmk = din('cmk', (2, 256, 512)); cmv = din('cmv', (2, 256, 512))
    memp = din('memp', (256, 1024))
    W32 = {n: din(n, WSPEC[n]) for n in WSPEC}
    b_forget = din('b_forget', (1, 8)); conv_w = din('conv_w', (12, 128))
    b_in_odd = din('b_in_odd', (16, 128)); gng = din('gng', (8, 128)); gnb = din('gnb', (8, 128))
    gws = din('gws', (8, 128, 128)); gbs = din('gbs', (1, 1024))
    ln_g = din('ln_g', (48, 128)); ln_b = din('ln_b', (48, 128))
    yp = dout('yp', (4096, 1024)); ys = dout('ys', (64, 1024))
    fkp = dout('fkp', (4096, 512)); fvp = dout('fvp', (4096, 512)); flp = dout('flp', (4096, 8))
    csp = dout('csp', (2, 512)); mkp = dout('mkp', (2, 256, 512)); mvp = dout('mvp', (2, 256, 512))
    fks = dout('fks', (64, 512)); fvs = dout('fvs', (64, 512)); fls = dout('fls', (64, 8))
    css = dout('css', (2, 512)); gvs = dout('gvs', (64, 1024))
    WS = {n: nc.dram_tensor('ws_' + n, [16, 128, 2048], BF16, kind="Internal").ap() for n in WSPEC if n[:2] not in ('mk', 'mv')}

    P = Prog(nc)
    es = contextlib.ExitStack()
    with es:
        sb = lambda name, shape, dt: es.enter_context(nc.sbuf_tensor(name, list(shape), dt))
        ident = sb('ident', (128, 128), F32); onesf = sb('onesf', (128, 128), F32)
        tri = sb('tri', (128, 128), F32); ones_b = sb('ones_b', (128, 128), BF16)
        mask_b = sb('mask_b', (128, 128), BF16)
        colsA = sb('colsA', (128, 128), F32); colsB = sb('colsB', (128, 128), F32)
        bfb = sb('bfb', (128, 8), F32); bsb = sb('bsb', (128, 8, 128), F32)
        wsT = sb('wsT', (128, 8, 128), BF16)
        memKT = [sb(f'memKT{l}', (128, 4, 256), BF16) for l in range(2)]
        memV = [sb(f'memV{l}', (128, 2, 512), BF16) for l in range(2)]
        xT = sb('xT', (128, 8, 512), F32); xbf = sb('xbf', (128, 8, 512), BF16)
        KT = sb('KT', (128, 4, 4096), BF16); Vs = sb('Vs', (128, 32, 4, 192), BF16)
        ctok = sb('ctok', (128, 33, 8), F32); carry = sb('carry', (128, 34, 8), F32)
        lf = sb('lf', (128, 33, 8), F32); biasb = sb('biasb', (128, 33, 8), F32)
        ctok_s = sb('ctok_s', (128, 9, 8), F32); carry_s = sb('carry_s', (128, 10, 8), F32); lf_s = sb('lf_s', (128, 9, 8), F32)
        lft = sb('lft', (128, 2, 8), F32)
        T2 = sb('T2', (128, 4, 8), F32)
        dmy = sb('dmy', (128, 8), F32)
        wring = [sb(f'wr{i}', (128, 2048), BF16) for i in range(NSLOT)]
        xin = [sb(f'xin{i}', (128, 1024), F32) for i in range(2)]
        stg = [sb(f'stg{i}', (128, 512), F32) for i in range(2)]
        rbuf = [sb(f'rb{i}', (128, 512), BF16) for i in range(2)]
        rsq = [sb(f'rs{i}', (128, 512), BF16) for i in range(2)]
        Pb = [sb(f'Pb{i}', (128, 512), BF16) for i in range(4)]
        eight_b = sb('eight_b', (128, 128), BF16)
        inv_b = sb('inv_b', (128, 128), BF16)
        rcb = [sb(f'rc{i}', (128, 128), F32) for i in range(2)]
        prebuf = sb('prebuf', (128, 4, 516), F32)
        work = sb('work', (128, WORK_BYTES // 2), BF16)
        PS = [es.enter_context(nc.psum_tensor(f'ps{i}', [128, 512], F32)) for i in range(8)]

        def wv(off, shape, dt):
            n = int(np.prod(shape))
            if dt == BF16:
                a = work[:, off // 2: off // 2 + n]
            else:
                a = work[:, off // 2: off // 2 + 2 * n].bitcast(F32)
            if len(shape) == 2:
                a = a.rearrange("p (a b) -> p a b", a=shape[0])
            return a

        def pg(off, nbytes):
            return [('wk', i) for i in range(off // PAGE, (off + nbytes - 1) // PAGE + 1)]

        psk = lambda b: ('ps', b)

        wstate = {'n': 0, 'stg': 0, 'prev_start': None, 'cast': 0}
        wseen = {}
        NSTG = 4
        vflat = Vs[:, :, :, :].rearrange("p a c d -> p (a c d)")
        wstg = [vflat[:, 4096 + i * 4096: 4096 + (i + 1) * 4096].bitcast(F32) for i in range(NSTG)]

        def wblock(name, c0, cw, k0=0, nk=None, keep=True):
            K, M = WSPEC[name]
            if nk is None:
                nk = K // 128
            n = nk * cw
            i = wstate['n'] % NSLOT
            wstate['n'] += 1
            dst = wring[i][:, 0:n].rearrange("p (a b) -> p a b", a=nk)
            bkey = (name, c0, k0)
            if bkey in wseen:
                bid = wseen[bkey]
                src = WS[name][bid, :, 0:n].rearrange("p (a b) -> p a b", a=nk)
                P.dma('sp', f'w{i}', lambda e: e.dma_start(out=dst, in_=src), r=[('scr', name, bid)], w=[('w', i)])
                wstate['prev_start'] = None
                return dst, ('w', i)
            bid = len([k for k in wseen if k[0] == name])
            wseen[bkey] = bid
            g = wstate['stg'] % NSTG
            wstate['stg'] += 1
            stage = wstg[g][:, 0:n]
            src = W32[name].rearrange("(kc p) m -> p kc m", p=128)[:, k0:k0 + nk, c0:c0 + cw]
            pos = wstate['prev_start']
            natural = len(P.ops)
            if pos is None:
                pos = natural
            ceng = 'dve'
            wstate['cast'] += 1
            P.op_at(pos, 'sp', lambda e: e.dma_start(out=stage.rearrange("p (a b) -> p a b", a=nk), in_=src), (), [('wstg', g)], dma=f'ws{g}')
            if ceng == 'dve':
                P.op_at(pos + 1, 'dve', lambda e: e.tensor_copy(out=wring[i][:, 0:n], in_=stage), [('wstg', g)], [('w', i)])
            else:
                P.op_at(pos + 1, 'act', lambda e: e.copy(out=wring[i][:, 0:n], in_=stage), [('wstg', g)], [('w', i)])
            nins = 2
            if keep:
                P.op_at(pos + 2, 'pool', lambda e: e.dma_start(out=WS[name][bid, :, 0:n], in_=wring[i][:, 0:n]), [('w', i)], [('scr', name, bid)], dma=f'wb{i}')
                nins = 3
            wstate['prev_start'] = natural + nins
            return dst, ('w', i)

        P.pool(lambda e: e.memset(onesf[:], 1.0), w=['onesf'])
        P.pool(lambda e: e.memset(ones_b[:], 1.0), w=['ones_b'])
        P.pool(lambda e: e.memset(eight_b[:], 0.0), w=['eight_b'])
        P.pool(lambda e: e.memset(eight_b[0:1, :], 8.0), r=['eight_b'], w=['eight_b'])
        P.pool(lambda e: e.memset(inv_b[:], 1.0 / 1024), w=['inv_b'])
        P.pool(lambda e: e.memset(dmy[:], 1.0), w=['dmy'])
        P.pool(lambda e: e.affine_select(out=ident[:], in_=onesf[:], pattern=[[-1, 128]], compare_op=ALU.is_equal,
                                         fill=0.0, base=0, channel_multiplier=1), r=['onesf'], w=['ident'])
        P.pool(lambda e: e.affine_select(out=tri[:], in_=onesf[:], pattern=[[1, 128]], compare_op=ALU.is_ge,
                                         fill=0.0, base=0, channel_multiplier=-1), r=['onesf'], w=['tri'])
        P.pool(lambda e: e.affine_select(out=mask_b[:], in_=ones_b[:], pattern=[[1, 128]], compare_op=ALU.is_ge,
                                         fill=0.0, base=0, channel_multiplier=-1), r=['ones_b'], w=['mask_b'])
        P.pool(lambda e: e.memset(carry[:, 0, :], 0.0), w=[('carryp', 0)])
        P.pool(lambda e: e.memset(carry_s[:, 0, :], 0.0), w=[('carrys', 0)])
        P.pool(lambda e: e.memset(prebuf[:, :, 0:2], 0.0), w=['prebuf'])
        P.pool(lambda e: e.memset(xin[0][:, 0:128], 0.0), w=['xin0'])
        P.pool(lambda e: e.memset(xin[1][:, 0:128], 0.0), w=['xin1'])
        P.dma('sp', 'xi0', lambda e: e.dma_start(out=xin[0][0:48, 0:128], in_=ln_g[:, :]), r=['xin0'], w=['xin0'])
        P.dma('sp', 'xi0', lambda e: e.dma_start(out=xin[0][48:96, 0:128], in_=ln_b[:, :]), r=['xin0'], w=['xin0'])
        P.dma('sp', 'xi1', lambda e: e.dma_start(out=xin[1][0:16, 0:128], in_=b_in_odd[:, :]), r=['xin1'], w=['xin1'])
        P.dma('sp', 'xi1', lambda e: e.dma_start(out=xin[1][16:24, 0:128], in_=gng[:, :]), r=['xin1'], w=['xin1'])
        P.dma('sp', 'xi1', lambda e: e.dma_start(out=xin[1][24:32, 0:128], in_=gnb[:, :]), r=['xin1'], w=['xin1'])
        P.dma('sp', 'xi1', lambda e: e.dma_start(out=xin[1][32:44, 0:128], in_=conv_w[:, :]), r=['xin1'], w=['xin1'])
        P.dma('sp', 'c_bfb', lambda e: e.dma_start(out=bfb[:], in_=b_forget[0:1, :].partition_broadcast(128)), w=['bfb'])
        P.dma('sp', 'c_bsb', lambda e: e.dma_start(out=bsb[:].rearrange("p a b -> p (a b)"), in_=gbs[0:1, :].partition_broadcast(128)), w=['bsb'])
        P.pe(lambda e: e.transpose(out=PS[0][:, 0:128], in_=xin[0][:, 0:128], identity=ident[:]), r=['xin0', 'ident'], w=[psk(0)])
        P.pe(lambda e: e.transpose(out=PS[1][:, 0:128], in_=xin[1][:, 0:128], identity=ident[:]), r=['xin1', 'ident'], w=[psk(1)])
        P.act(lambda e: e.copy(out=colsA[:], in_=PS[0][:, 0:128]), r=[psk(0)], w=['colsA'])
        P.act(lambda e: e.copy(out=colsB[:], in_=PS[1][:, 0:128]), r=[psk(1)], w=['colsB'])
        lng = lambda l, k, c: colsA[:, (l * 3 + k) * 8 + c:(l * 3 + k) * 8 + c + 1]
        lnb = lambda l, k, c: colsA[:, 48 + (l * 3 + k) * 8 + c:48 + (l * 3 + k) * 8 + c + 1]
        bincol = lambda c: colsB[:, c:c + 1]
        gngcol = lambda c: colsB[:, 16 + c:17 + c]
        gnbcol = lambda c: colsB[:, 24 + c:25 + c]
        cwcol = lambda j, cc: colsB[:, 32 + j * 4 + cc:33 + j * 4 + cc]
        for g in range(8):
            st = xin[g % 2]
            k = f'xin{g % 2}'
            P.dma('sp', f'xi{g % 2}', lambda e, st=st, g=g: e.dma_start(out=st[:, 0:128], in_=gws[g, :, :]), w=[k])
            P.pool(lambda e, st=st: e.affine_select(out=st[:, 0:128], in_=st[:, 0:128], pattern=[[-1, 128]], compare_op=ALU.is_ge,
                                                     fill=0.0, base=0, channel_multiplier=1), r=[k], w=[k])
            P.pe(lambda e, st=st, g=g: e.transpose(out=PS[g % 2][:, 0:128], in_=st[:, 0:128], identity=ident[:]), r=[k, 'ident'], w=[psk(g % 2)])
            P.act(lambda e, g=g: e.copy(out=wsT[:, g, :], in_=PS[g % 2][:, 0:128]), r=[psk(g % 2)], w=['wsT'])

        mst = wv(0, (2, 1024), F32)
        memT = wv(8192, (8, 256), BF16)
        P.dma('sp', 'c_mst', lambda e: e.dma_start(out=mst, in_=memp.rearrange("(a p) d -> p a d", p=128)), w=pg(0, 8192))
        for mb in range(2):
            for half in range(2):
                bank = (mb * 2 + half) % 2
                for q in range(4):
                    c = half * 4 + q
                    P.pe(lambda e, mb=mb, c=c, q=q, bank=bank: e.transpose(out=PS[bank][:, q * 128:(q + 1) * 128], in_=mst[:, mb, c * 128:(c + 1) * 128], identity=ident[:]),
                         r=pg(0, 8192) + ['ident'], w=[psk(bank)])
                P.act(lambda e, mb=mb, half=half, bank=bank: e.copy(out=memT[:, half * 4:half * 4 + 4, mb * 128:(mb + 1) * 128],
                                                                      in_=PS[bank][:, :].rearrange("p (a b) -> p a b", a=4)),
                      r=[psk(bank)], w=pg(8192, 4096))
        mrk = pg(8192, 4096)
        stq = {'n': 0}

        def stage_out(ps_ap, rows, cols, dram_ap, psr, extra=None):
            i = stq['n'] % 2
            stq['n'] += 1
            P.act(lambda e: e.copy(out=stg[i][0:rows, 0:cols], in_=ps_ap), r=psr, w=[('stg', i)])
            P.dma('act', f'so{i}', lambda e: e.dma_start(out=dram_ap, in_=stg[i][0:rows, 0:cols]), r=[('stg', i)])

        for l in range(2):
            for nm, dst in (('mk', mkp), ('mv', mvp)):
                blks = [wblock(f'{nm}{l}', j * 256, 256, keep=False) for j in range(2)]
                if nm == 'mk':
                    for h in range(4):
                        wvw, wk = blks[h // 2]
                        bank = h % 2
                        for kc in range(8):
                            P.pe(lambda e, wvw=wvw, kc=kc, h=h, bank=bank: e.matmul(PS[bank][:, 0:256], lhsT=wvw[:, kc, (h % 2) * 128:(h % 2) * 128 + 128],
                                                                                     rhs=memT[:, kc, :], start=(kc == 0), stop=(kc == 7)),
                                 r=[wk] + mrk, w=[psk(bank)])
                        P.act(lambda e, h=h, bank=bank, l=l: e.copy(out=memKT[l][:, h, :], in_=PS[bank][:, 0:256]), r=[psk(bank)], w=[('memKT', l)])
                for mb in range(2):
                    bank = 2 + mb
                    for j in range(2):
                        wvw, wk = blks[j]
                        for kc in range(8):
                            P.pe(lambda e, wvw=wvw, kc=kc, mb=mb, j=j, bank=bank: e.matmul(PS[bank][:, j * 256:(j + 1) * 256], lhsT=memT[:, kc, mb * 128:(mb + 1) * 128],
                                                                                            rhs=wvw[:, kc, :], start=(kc == 0), stop=(kc == 7)),
                                 r=[wk] + mrk, w=[psk(bank)])
                    if nm == 'mv':
                        P.dve(lambda e, mb=mb, bank=bank, l=l: e.tensor_copy(out=memV[l][:, mb, :], in_=PS[bank][:, :]), r=[psk(bank)], w=[('memV', l)])
                    stage_out(PS[bank][:, :], 128, 512, dst[l, mb * 128:(mb + 1) * 128, :], [psk(bank)])

        mmq = {'n': 0}

        def mmbank():
            b = mmq['n'] % 4
            mmq['n'] += 1
            return b

        def ln_apply(NT, src_of, dstbf_of, gcol, bcol, src_keys, defer=True):
            o_tmp, o_A = 24576, 26624
            tmp = wv(o_tmp, (512,), F32); A = wv(o_A, (512,), F32)
            P.act(lambda e: e.activation(out=tmp[:, 0:NT], in_=PS[6][:, 0:NT], func=AF.Square), r=[psk(6)], w=pg(o_tmp, 2048))
            P.dve(lambda e: e.tensor_tensor(out=tmp[:, 0:NT], in0=PS[7][:, 0:NT], in1=tmp[:, 0:NT], op=ALU.subtract), r=[psk(7)] + pg(o_tmp, 2048), w=pg(o_tmp, 2048))
            P.act(lambda e: e.activation(out=tmp[:, 0:NT], in_=tmp[:, 0:NT], func=AF.Ln, bias=EPS), r=pg(o_tmp, 2048), w=pg(o_tmp, 2048))
            P.act(lambda e: e.activation(out=A[:, 0:NT], in_=tmp[:, 0:NT], func=AF.Exp, scale=-0.5), r=pg(o_tmp, 2048), w=pg(o_A, 2048))

            def sub(c):
                P.dve(lambda e: e.tensor_tensor(out=src_of(c), in0=src_of(c), in1=PS[6][:, 0:NT], op=ALU.subtract), r=src_keys(c) + [psk(6)], w=src_keys(c))
            sub(0); sub(1)
            for c in range(8):
                P.dve(lambda e, c=c: e.tensor_tensor(out=src_of(c), in0=src_of(c), in1=A[:, 0:NT], op=ALU.mult), r=src_keys(c) + pg(o_A, 2048), w=src_keys(c))
                if dstbf_of is not None:
                    P.act(lambda e, c=c: e.activation(out=dstbf_of(c), in_=src_of(c), func=AF.Identity, bias=bcol(c), scale=gcol(c)), r=src_keys(c), w=[('xbf', c)])
                if dstbf_of is None or not defer:
                    P.act(lambda e, c=c: e.activation(out=src_of(c), in_=src_of(c), func=AF.Identity, bias=bcol(c), scale=gcol(c)), r=src_keys(c), w=src_keys(c))
                if c + 2 < 8:
                    sub(c + 2)
            if dstbf_of is not None and defer:
                def fin_fp32():
                    for c in range(8):
                        P.act(lambda e, c=c: e.activation(out=src_of(c), in_=src_of(c), func=AF.Identity, bias=bcol(c), scale=gcol(c)), r=src_keys(c), w=src_keys(c))
                pending_fin.append(fin_fp32)

        pending_fin = []

        def flush_fin():
            while pending_fin:
                pending_fin.pop(0)()

        def stats_ew(NT, c, src_ap, src_keys):
            rb = rbuf[c % 2]; rs = rsq[c % 2]
            P.dve(lambda e: e.tensor_copy(out=rb[:, 0:NT], in_=src_ap), r=src_keys, w=[('rb', c % 2)])
            P.act(lambda e: e.activation(out=rs[:, 0:NT], in_=src_ap, func=AF.Square), r=src_keys, w=[('rs', c % 2)])

        def stats_pe(NT, c):
            rb = rbuf[c % 2]; rs = rsq[c % 2]
            P.pe(lambda e: e.matmul(PS[6][:, 0:NT], lhsT=inv_b[:, :], rhs=rb[:, 0:NT], start=(c == 0), stop=(c == 7)), r=[('rb', c % 2), 'inv_b'], w=[psk(6)])
            P.pe(lambda e: e.matmul(PS[7][:, 0:NT], lhsT=inv_b[:, :], rhs=rs[:, 0:NT], start=(c == 0), stop=(c == 7)), r=[('rs', c % 2), 'inv_b'], w=[psk(7)])

        def proj_postnorm(NT, l, k, mm_for_chunk, need_bf=True, defer=True):
            flush_fin()
            pend = None
            for m in range(8):
                bank = mmbank()
                mm_for_chunk(m, bank)
                if pend is not None:
                    stats_pe(NT, pend)
                P.dve(lambda e, m=m, bank=bank: e.scalar_tensor_tensor(out=xT[:, m, 0:NT], in0=xT[:, m, 0:NT], scalar=ALPHA, in1=PS[bank][:, 0:NT],
                                                                       op0=ALU.mult, op1=ALU.add), r=[psk(bank), ('xT', m)], w=[('xT', m)])
                stats_ew(NT, m, xT[:, m, 0:NT], [('xT', m)])
                pend = m
            stats_pe(NT, pend)
            ln_apply(NT, lambda c: xT[:, c, 0:NT], (lambda c: xbf[:, c, 0:NT]) if need_bf else None, lambda c: lng(l, k, c), lambda c: lnb(l, k, c), lambda c: [('xT', c)], defer=defer)

        def simple_proj(NT, name, nk, src_of, src_keys, l, k):
            cw = 2048 // nk
            per_blk = cw // 128
            st = {}

            def mm(m, bank):
                if m % per_blk == 0:
                    st['blk'] = wblock(name, m * 128, cw)
                wvw, wk = st['blk']
                mo = (m % per_blk) * 128
                for kc in range(nk):
                    P.pe(lambda e, kc=kc: e.matmul(PS[bank][:, 0:NT], lhsT=wvw[:, kc, mo:mo + 128], rhs=src_of(kc), start=(kc == 0), stop=(kc == nk - 1)),
                         r=[wk] + src_keys(kc), w=[psk(bank)])
            proj_postnorm(NT, l, k, mm, defer=(k != 0))

        xdone = set()

        def xload(tj, sample, sub):
            if (tj, sample, sub) in xdone:
                return
            xdone.add((tj, sample, sub))
            sw_ = 64 if sample else 128
            t0_ = 0 if sample else tj * 512
            src_ = xs if sample else xp
            xi = xin[sub % 2]
            P.dma('sp', f'xi{sub % 2}', lambda e: e.dma_start(out=xi[0:sw_, :], in_=src_[t0_ + sub * 128: t0_ + sub * 128 + sw_, :]), w=[f'xin{sub % 2}'])

        def xprefetch(tj, sample):
            if sample:
                return
            if tj + 1 < nprompt:
                xload(tj + 1, False, 0); xload(tj + 1, False, 1)
            elif do_sample:
                xload(0, True, 0)

        def tile_pass(tj, sample):
            NT = 64 if sample else 512
            nsub = 1 if sample else 4
            sw = 64 if sample else 128
            tok0 = 0 if sample else tj * 512
            xsrc = xs if sample else xp
            blk0 = 8 if sample else tj * 4
            kcol0 = 1024 if sample else tok0
            lf_ = lf_s if sample else lf; ctok_ = ctok_s if sample else ctok; carry_ = carry_s if sample else carry
            ck = 's' if sample else 'p'
            for sub in range(nsub):
                xi = xin[sub % 2]; xk = f'xin{sub % 2}'
                xload(tj, sample, sub)
                for half in range(2):
                    bank = mmbank()
                    for q in range(4):
                        c = half * 4 + q
                        P.pe(lambda e, xi=xi, c=c, q=q, bank=bank: e.transpose(out=PS[bank][:, q * 128:q * 128 + sw], in_=xi[0:sw, c * 128:(c + 1) * 128], identity=ident[0:sw, 0:sw]),
                             r=[xk, 'ident'], w=[psk(bank)])
                    srcv = PS[bank][:, :].rearrange("p (a b) -> p a b", a=4)[:, :, 0:sw]
                    P.dve(lambda e, half=half, sub=sub, srcv=srcv: e.tensor_copy(out=xbf[:, half * 4:half * 4 + 4, sub * 128:sub * 128 + sw], in_=srcv),
                          r=[psk(bank)], w=[('xbf', half * 4 + q) for q in range(4)])
                    P.act(lambda e, half=half, sub=sub, srcv=srcv: e.copy(out=xT[:, half * 4:half * 4 + 4, sub * 128:sub * 128 + sw], in_=srcv),
                          r=[psk(bank)], w=[('xT', half * 4 + q) for q in range(4)])
            xbk = [('xbf', c) for c in range(8)]

            def mm_interleaved(accs):
                for kc in range(8):
                    for (bank, lfn, wk) in accs:
                        P.pe(lambda e, kc=kc, bank=bank, lfn=lfn: e.matmul(PS[bank][:, 0:NT], lhsT=lfn(kc), rhs=xbf[:, kc, 0:NT], start=(kc == 0), stop=(kc == 7)),
                             r=[wk, ('xbf', kc)], w=[psk(bank)])

            def emit_y():
                flush_fin()
                ydst = ys if sample else yp
                for sub in range(nsub):
                    for half in range(2):
                        bank = mmbank()
                        for q in range(4):
                            c = half * 4 + q
                            P.pe(lambda e, c=c, q=q, sub=sub, bank=bank: e.transpose(out=PS[bank][0:sw, q * 128:(q + 1) * 128], in_=xT[:, c, sub * 128:sub * 128 + sw], identity=ident[:]),
                                 r=[('xT', c), 'ident'], w=[psk(bank)])
                        stage_out(PS[bank][0:sw, :], sw, 512, ydst[tok0 + sub * 128: tok0 + sub * 128 + sw, half * 512:(half + 1) * 512], [psk(bank)])
            if dbg <= 1:
                emit_y(); return

            o_QT, o_hT, o_bg, o_cat, o_ca, o_RQ = 0, 8192, 16384, 24576, 32768, 8192
            QTz = wv(o_QT, (8, 512), BF16); hT = wv(o_hT, (4, 512), F32); bgT = wv(o_bg, (4, 512), F32)
            cat = wv(o_cat, (8, 512), BF16)
            ca = wv(o_ca, (512,), F32)
            ca4 = wv(o_hT, (4, 512), F32)
            if sample:
                kst = wv(0, (8, 512), F32)
                P.dma('pool', 'c_kst', lambda e: e.dma_start(out=kst, in_=cfk.rearrange("(a p) d -> p a d", p=128)), w=pg(0, 16384))
                for b in range(8):
                    for par in range(2):
                        P.dma('pool', f'c_vc{b}_{par}', lambda e, b=b, par=par: e.dma_start(out=Vs[:, b, :, par * 128:par * 128 + 64],
                              in_=cfv[b * 128:(b + 1) * 128, :].rearrange("p (c e d) -> p c e d", c=4, e=2)[:, :, par, :]), w=[('V', b, par)])
                P.dma('pool', 'c_lf', lambda e: e.dma_start(out=lf_[:, 0:8, :], in_=cfl.rearrange("(a p) d -> p a d", p=128)), w=[('lf' + ck, b) for b in range(8)])
                for cc in range(4):
                    P.dma('pool', 'c_pre', lambda e, cc=cc: e.dma_start(out=prebuf[:, cc, 0:2], in_=sconv[:, cc * 128:(cc + 1) * 128].rearrange("j p -> p j"), allow_slow_non_contiguous=True),
                          r=['prebuf'], w=['prebuf'])
                for b in range(8):
                    bank = mmbank()
                    for c in range(4):
                        P.pe(lambda e, b=b, c=c, bank=bank: e.transpose(out=PS[bank][:, c * 128:(c + 1) * 128], in_=kst[:, b, c * 128:(c + 1) * 128], identity=ident[:]),
                             r=pg(0, 16384) + ['ident'], w=[psk(bank)])
                    P.act(lambda e, b=b, bank=bank: e.copy(out=KT[:, :, b * 128:(b + 1) * 128], in_=PS[bank][:, :].rearrange("p (a b) -> p a b", a=4)),
                          r=[psk(bank)], w=[('KT', c, b // 4) for c in range(4)])
                mks = wv(16384, (2, 512), F32)
                for l in range(2):
                    P.dma('pool', 'c_mks', lambda e, l=l: e.dma_start(out=mks, in_=cmk[l].rearrange("(a p) d -> p a d", p=128)), w=pg(16384, 4096))
                    P.dma('pool', f'c_mv{l}', lambda e, l=l: e.dma_start(out=memV[l][:, :, :], in_=cmv[l].rearrange("(a p) d -> p a d", p=128)), w=[('memV', l)])
                    for mb in range(2):
                        bank = mmbank()
                        for h in range(4):
                            P.pe(lambda e, mb=mb, h=h, bank=bank: e.transpose(out=PS[bank][:, h * 128:(h + 1) * 128], in_=mks[:, mb, h * 128:(h + 1) * 128], identity=ident[:]),
                                 r=pg(16384, 4096) + ['ident'], w=[psk(bank)])
                        P.act(lambda e, mb=mb, bank=bank, l=l: e.copy(out=memKT[l][:, :, mb * 128:(mb + 1) * 128], in_=PS[bank][:, :].rearrange("p (a b) -> p a b", a=4)),
                              r=[psk(bank)], w=[('memKT', l)])
            wvw, wk = wblock('win0', 1536, 8)
            for sub in range(nsub):
                blk = blk0 + sub
                bank = mmbank()
                for kc in range(8):
                    P.pe(lambda e, wvw=wvw, kc=kc, sub=sub, bank=bank: e.matmul(PS[bank][0:sw, 0:8], lhsT=xbf[:, kc, sub * 128:sub * 128 + sw], rhs=wvw[:, kc, :],
                                                                                 start=(kc == 0), stop=(kc == 7)), r=[wk, ('xbf', kc)], w=[psk(bank)])
                t = lft[:, sub % 2, :]; tk = ('lft', sub % 2)
                P.dve(lambda e, t=t, bank=bank: e.tensor_tensor(out=t[0:sw, :], in0=PS[bank][0:sw, 0:8], in1=bfb[0:sw, :], op=ALU.add), r=[psk(bank), 'bfb'], w=[tk])
                P.act(lambda e, t=t: e.activation(out=t[0:sw, :], in_=t[0:sw, :], func=AF.Exp, scale=-1.0), r=[tk], w=[tk])
                P.act(lambda e, t=t: e.activation(out=t[0:sw, :], in_=t[0:sw, :], func=AF.Ln, bias=1.0), r=[tk], w=[tk])
                P.dve(lambda e, t=t, blk=blk: e.tensor_scalar(out=lf_[0:sw, blk, :], in0=t[0:sw, :], scalar1=-1.0, scalar2=None, op0=ALU.mult), r=[tk], w=[('lf' + ck, blk)])
                dst = (fls if sample else flp)[tok0 + sub * 128: tok0 + sub * 128 + sw, :]
                P.dma('act', 'solf', lambda e, blk=blk, dst=dst: e.dma_start(out=dst, in_=lf_[0:sw, blk, :]), r=[('lf' + ck, blk)])
            P.dve(lambda e: e.memset(QTz[:, :, :], 0.0), w=pg(o_QT, 8192))
            for part, base in (('q', 0), ('k', 512)):
                for j in range(2):
                    wvw, wk = wblock('win0', base + j * 256, 256)
                    for mm_ in range(2):
                        c = j * 2 + mm_
                        bank = mmbank()
                        for kc in range(8):
                            P.pe(lambda e, wvw=wvw, kc=kc, mm_=mm_, bank=bank: e.matmul(PS[bank][:, 0:NT], lhsT=wvw[:, kc, mm_ * 128:(mm_ + 1) * 128], rhs=xbf[:, kc, 0:NT],
                                                                                         start=(kc == 0), stop=(kc == 7)), r=[wk, ('xbf', kc)], w=[psk(bank)])
                        if part == 'q':
                            P.act(lambda e, c=c, bank=bank: e.copy(out=QTz[0:64, 2 * c, 0:NT], in_=PS[bank][0:64, 0:NT]), r=[psk(bank)], w=pg(o_QT + 2 * c * 1024, 1024))
                            P.act(lambda e, c=c, bank=bank: e.copy(out=QTz[64:128, 2 * c + 1, 0:NT], in_=PS[bank][64:128, 0:NT]), r=[psk(bank)], w=pg(o_QT + (2 * c + 1) * 1024, 1024))
                        else:
                            P.act(lambda e, c=c, bank=bank: e.copy(out=KT[:, c, kcol0:kcol0 + NT], in_=PS[bank][:, 0:NT]), r=[psk(bank)],
                                  w=[('KT', c, kcol0 // 512)])
                    if part == 'k':
                        for sub in range(nsub):
                            bank = mmbank()
                            for kc in range(8):
                                P.pe(lambda e, wvw=wvw, kc=kc, sub=sub, bank=bank: e.matmul(PS[bank][0:sw, 0:256], lhsT=xbf[:, kc, sub * 128:sub * 128 + sw], rhs=wvw[:, kc, :],
                                                                                             start=(kc == 0), stop=(kc == 7)), r=[wk, ('xbf', kc)], w=[psk(bank)])
                            dst = (fks if sample else fkp)[tok0 + sub * 128: tok0 + sub * 128 + sw, j * 256:(j + 1) * 256]
                            stage_out(PS[bank][0:sw, 0:256], sw, 256, dst, [psk(bank)])
            vb0, vb1 = (0, 9) if sample else (blk0, blk0 + 4)
            P.dve(lambda e: e.memset(Vs[:, vb0:vb1, :, 64:128].rearrange("p a c d -> p (a c) d"), 1.0), w=[('Vones', b) for b in range(vb0, vb1)])
            for j in range(2):
                wvw, wk = wblock('win0', 1024 + j * 256, 256)
                for sub in range(nsub):
                    bank = mmbank()
                    for kc in range(8):
                        P.pe(lambda e, wvw=wvw, kc=kc, sub=sub, bank=bank: e.matmul(PS[bank][0:sw, 0:256], lhsT=xbf[:, kc, sub * 128:sub * 128 + sw], rhs=wvw[:, kc, :],
                                                                                     start=(kc == 0), stop=(kc == 7)), r=[wk, ('xbf', kc)], w=[psk(bank)])
                    for par in range(2):
                        P.dve(lambda e, sub=sub, j=j, bank=bank, par=par: e.tensor_copy(out=Vs[0:sw, blk0 + sub, 2 * j:2 * j + 2, par * 128:par * 128 + 64],
                                                                                      in_=PS[bank][0:sw, 0:256].rearrange("p (c e d) -> p c e d", c=2, e=2)[:, :, par, :]),
                              r=[psk(bank)], w=[('V', blk0 + sub, par)])
                    dst = (fvs if sample else fvp)[tok0 + sub * 128: tok0 + sub * 128 + sw, j * 256:(j + 1) * 256]
                    stage_out(PS[bank][0:sw, 0:256], sw, 256, dst, [psk(bank)])
            cblocks = list(range(0, 9)) if sample else [blk0 + s for s in range(nsub)]
            for blk in cblocks:
                w_ = 64 if (sample and blk == 8) else 128
                bank = mmbank()
                P.pe(lambda e, blk=blk, w_=w_, bank=bank: e.matmul(PS[bank][0:w_, 0:8], lhsT=tri[0:w_, 0:w_], rhs=lf_[0:w_, blk, :], start=True, stop=True),
                     r=[('lf' + ck, blk), 'tri'], w=[psk(bank)])
                P.pe(lambda e, blk=blk, w_=w_, bank=bank: e.matmul(PS[bank][:, 8:16], lhsT=onesf[0:w_, :], rhs=lf_[0:w_, blk, :], start=True, stop=True),
                     r=[('lf' + ck, blk), 'onesf'], w=[psk(bank)])
                P.dve(lambda e, blk=blk, w_=w_, bank=bank: e.tensor_tensor(out=ctok_[0:w_, blk, :], in0=PS[bank][0:w_, 0:8], in1=carry_[0:w_, blk, :], op=ALU.add),
                      r=[psk(bank), ('carry' + ck, blk)], w=[('ctok' + ck, blk)])
                P.dve(lambda e, blk=blk, bank=bank: e.tensor_tensor(out=carry_[:, blk + 1, :], in0=PS[bank][:, 8:16], in1=carry_[:, blk, :], op=ALU.add),
                      r=[psk(bank), ('carry' + ck, blk)], w=[('carry' + ck, blk + 1)])
            if dbg <= 2:
                emit_y(); return
            for j in range(6):
                wvw, wk = wblock('win0', 1544 + j * 256, 256)
                for mm_ in range(2):
                    cc = (j % 2) * 2 + mm_
                    bank = mmbank()
                    for kc in range(8):
                        P.pe(lambda e, wvw=wvw, kc=kc, mm_=mm_, bank=bank: e.matmul(PS[bank][:, 0:NT], lhsT=wvw[:, kc, mm_ * 128:(mm_ + 1) * 128], rhs=xbf[:, kc, 0:NT],
                                                                                     start=(kc == 0), stop=(kc == 7)), r=[wk, ('xbf', kc)], w=[psk(bank)])
                    if j < 2:
                        P.act(lambda e, cc=cc, bank=bank: e.copy(out=hT[:, cc, 0:NT], in_=PS[bank][:, 0:NT]), r=[psk(bank)], w=pg(o_hT + cc * 2048, 2048))
                    elif j < 4:
                        P.act(lambda e, cc=cc, bank=bank: e.copy(out=bgT[:, cc, 0:NT], in_=PS[bank][:, 0:NT]), r=[psk(bank)], w=pg(o_bg + cc * 2048, 2048))
                    else:
                        P.dve(lambda e, cc=cc, bank=bank: e.tensor_tensor(out=prebuf[:, cc, 2:2 + NT], in0=PS[bank][:, 0:NT], in1=hT[:, cc, 0:NT], op=ALU.mult),
                              r=[psk(bank)] + pg(o_hT + cc * 2048, 2048), w=['prebuf'])
                        P.act(lambda e, cc=cc: e.activation(out=ca4[:, cc, 0:NT], in_=prebuf[:, cc, 0:NT], func=AF.Copy, scale=cwcol(0, cc)),
                              r=['prebuf', 'colsB'], w=pg(o_hT + cc * 2048, 2048))
            if dbg <= 3:
                emit_y(); return
            nblk = blk0 + nsub
            endidx = blk0 + nsub
            for b in range(nblk):
                kw = 64 if (sample and b == 8) else 128
                P.dve(lambda e, b=b, kw=kw: e.tensor_tensor(out=biasb[0:kw, b, :], in0=carry_[0:kw, endidx, :], in1=ctok_[0:kw, b, :], op=ALU.subtract),
                      r=[('carry' + ck, endidx), ('ctok' + ck, b)], w=[('bias', b)])
            if not sample:
                for i in range(4):
                    P.dve(lambda e, i=i: e.tensor_tensor(out=T2[0:1, i, :], in0=carry_[0:1, blk0 + i + 1, :], in1=carry_[0:1, endidx, :], op=ALU.subtract),
                          r=[('carry' + ck, blk0 + i + 1), ('carry' + ck, endidx)], w=['T2'])
                for h in range(8):
                    P.dve(lambda e, h=h: e.tensor_copy(out=xbf[0:1, h, :].rearrange("p (i j) -> p i j", i=4), in_=T2[0:1, :, h:h + 1].to_broadcast([1, 4, 128])),
                          r=['T2'], w=[('xbf', h)])
            for cc in range(4):
                kc_ = pg(o_hT + cc * 2048, 2048)
                P.dve(lambda e, cc=cc: e.scalar_tensor_tensor(out=ca4[:, cc, 0:NT], in0=prebuf[:, cc, 1:1 + NT], scalar=cwcol(1, cc), in1=ca4[:, cc, 0:NT], op0=ALU.mult, op1=ALU.add),
                      r=['prebuf', 'colsB'] + kc_, w=kc_)
                P.dve(lambda e, cc=cc: e.scalar_tensor_tensor(out=ca4[:, cc, 0:NT], in0=prebuf[:, cc, 2:2 + NT], scalar=cwcol(2, cc), in1=ca4[:, cc, 0:NT], op0=ALU.mult, op1=ALU.add),
                      r=['prebuf', 'colsB'] + kc_, w=kc_)
                P.dve(lambda e, cc=cc: e.tensor_tensor(out=cat[:, 4 + cc, 0:NT], in0=ca4[:, cc, 0:NT], in1=bgT[:, cc, 0:NT], op=ALU.mult),
                      r=kc_ + pg(o_bg + cc * 2048, 2048), w=pg(o_cat + (4 + cc) * 1024, 1024))
            last_prompt = (not sample) and tj == nprompt - 1
            if sample or last_prompt:
                dsto = (css if sample else csp)
                for cc in range(4):
                    P.dma('act', 'socs', lambda e, cc=cc: e.dma_start(out=dsto[:, cc * 128:(cc + 1) * 128].rearrange("j p -> p j"), in_=prebuf[:, cc, NT:NT + 2], allow_slow_non_contiguous=True), r=['prebuf'])
            else:
                P.dve(lambda e: e.tensor_copy(out=prebuf[:, :, 0:2], in_=prebuf[:, :, NT:NT + 2]), r=['prebuf'], w=['prebuf'])
            steps = []
            for c in range(4):
                for b in range(nblk):
                    for e_ in range(2):
                        steps.append((c, e_, b))
            LOOK = 3
            nst = len(steps)
            rcF = wv(o_ca, (512,), F32)

            def geom(b):
                kw = 64 if (sample and b == 8) else 128
                d = b - blk0
                qlo = 128 * d if d > 0 else 0
                return kw, d, qlo

            def emit_qk(n):
                c, e_, b = steps[n]
                kw, d, qlo = geom(b)
                sl = n % 4
                sbk = (0, 1, 6, 7)[sl]
                h = 2 * c + e_
                Sap = PS[sbk][0:kw, qlo:NT]
                aug = (not sample) and qlo < 384
                P.pe(lambda e: e.matmul(Sap, lhsT=KT[:, c, b * 128:b * 128 + kw], rhs=QTz[:, h, qlo:NT], start=True, stop=(not aug)),
                     r=[('KT', c, (b * 128) // 512)] + pg(o_QT + h * 1024, 1024), w=[psk(sbk)])
                if aug:
                    P.pe(lambda e: e.matmul(PS[sbk][0:kw, qlo:384], lhsT=eight_b[:, 0:kw], rhs=xbf[:, h, qlo:384], start=False, stop=True),
                         r=['eight_b', ('xbf', h)], w=[psk(sbk)])
                P.act(lambda e: e.activation(out=Pb[sl][0:kw, qlo:NT], in_=Sap, func=AF.Exp, bias=biasb[0:kw, b, h:h + 1], scale=0.125),
                      r=[psk(sbk), ('bias', b)], w=[('P', sl)])
                if d >= 0:
                    qwm = min(128, NT - qlo)
                    P.dve(lambda e: e.tensor_tensor(out=Pb[sl][0:kw, qlo:qlo + qwm], in0=Pb[sl][0:kw, qlo:qlo + qwm], in1=mask_b[0:kw, 0:qwm], op=ALU.mult),
                          r=[('P', sl), 'mask_b'], w=[('P', sl)])

            def accbank(c, e_):
                return 2 + (c % 2) * 2 + e_

            def emit_pv(n):
                c, e_, b = steps[n]
                kw, d, qlo = geom(b)
                sl = n % 4
                h = 2 * c + e_
                bk = accbank(c, e_)
                lw = Vs[0:kw, b, c, e_ * 64:e_ * 64 + 128]
                P.pe(lambda e: e.matmul(PS[bk][:, qlo:NT], lhsT=lw, rhs=Pb[sl][0:kw, qlo:NT], start=(b == 0), stop=(b == nblk - 1)),
                     r=[('V', b, e_), ('P', sl), ('Vones', b)], w=[psk(bk)])
                if e_ == 1 and b == nblk - 1:
                    for ee in range(2):
                        bk2 = accbank(c, ee)
                        po, pd = (0, 64) if ee == 0 else (64, 0)
                        if False:
                            P.act(lambda e, bk2=bk2, pd=pd: e.activation(out=rcF[pd:pd + 64, 0:NT], in_=PS[bk2][pd:pd + 64, 0:NT], func=AF.Ln), r=[psk(bk2)], w=pg(o_ca, 2048))
                            P.act(lambda e, pd=pd: e.activation(out=rcF[pd:pd + 64, 0:NT], in_=rcF[pd:pd + 64, 0:NT], func=AF.Exp, scale=-1.0), r=pg(o_ca, 2048), w=pg(o_ca, 2048))
                        else:
                            P.dve(lambda e, bk2=bk2, pd=pd: e.reciprocal(out=rcF[pd:pd + 64, 0:NT], in_=PS[bk2][pd:pd + 64, 0:NT]), r=[psk(bk2)], w=pg(o_ca, 2048))
                        P.dve(lambda e, bk2=bk2, po=po, pd=pd: e.tensor_tensor(out=cat[po:po + 64, c, 0:NT], in0=PS[bk2][po:po + 64, 0:NT], in1=rcF[pd:pd + 64, 0:NT], op=ALU.mult),
                              r=[psk(bk2)] + pg(o_ca, 2048), w=pg(o_cat + c * 1024, 1024))

            for n in range(nst + LOOK):
                if n < nst:
                    emit_qk(n)
                if n - LOOK >= 0:
                    emit_pv(n - LOOK)
            simple_proj(NT, 'wout0', 8, lambda kc: cat[:, kc, 0:NT], lambda kc: pg(o_cat + kc * 1024, 1024), 0, 0)

            def mem_attn(l):
                o_QM, o_PM, o_OM, o_rc = 0, 4096, 8192, 12288
                QM = wv(o_QM, (4, 512), BF16); PM = wv(o_PM, (4, 512), BF16); OM = wv(o_OM, (4, 512), BF16)
                rcm = [wv(o_rc + i * 2048, (512,), F32) for i in range(2)]
                qblks = [wblock(f'mq{l}', j * 256, 256) for j in range(2)]
                qbanks = [mmbank() for _ in range(4)]
                mm_interleaved([(qbanks[h], (lambda kc, h=h: qblks[h // 2][0][:, kc, (h % 2) * 128:(h % 2) * 128 + 128]), qblks[h // 2][1]) for h in range(4)])
                for h in range(4):
                    P.dve(lambda e, h=h: e.tensor_copy(out=QM[:, h, 0:NT], in_=PS[qbanks[h]][:, 0:NT]), r=[psk(qbanks[h])], w=pg(o_QM + h * 1024, 1024))
                flush_fin()
                def fin(h):
                    s0 = (h % 2) * 4
                    rc = rcm[h % 2]; rk = pg(o_rc + (h % 2) * 2048, 2048)
                    P.act(lambda e: e.activation(out=rc[:, 0:NT], in_=PS[s0 + 3][:, 0:NT], func=AF.Ln), r=[psk(s0 + 3)], w=rk)
                    P.act(lambda e: e.activation(out=rc[:, 0:NT], in_=rc[:, 0:NT], func=AF.Exp, scale=-1.0), r=rk, w=rk)
                    P.dve(lambda e: e.tensor_tensor(out=OM[:, h, 0:NT], in0=PS[s0 + 2][:, 0:NT], in1=rc[:, 0:NT], op=ALU.mult),
                          r=[psk(s0 + 2)] + rk, w=pg(o_OM + h * 1024, 1024))

                def scores(h):
                    s0 = (h % 2) * 4
                    for mb in range(2):
                        P.pe(lambda e, mb=mb: e.matmul(PS[s0 + mb][:, 0:NT], lhsT=memKT[l][:, h, mb * 128:(mb + 1) * 128], rhs=QM[:, h, 0:NT], start=True, stop=True),
                             r=[('memKT', l)] + pg(o_QM + h * 1024, 1024), w=[psk(s0 + mb)])
                        P.act(lambda e, mb=mb: e.activation(out=PM[:, (h % 2) * 2 + mb, 0:NT], in_=PS[s0 + mb][:, 0:NT], func=AF.Exp, scale=float(128 ** -0.5)),
                              r=[psk(s0 + mb)], w=pg(o_PM + ((h % 2) * 2 + mb) * 1024, 1024))

                def pv(h):
                    s0 = (h % 2) * 4
                    for mb in range(2):
                        pk = pg(o_PM + ((h % 2) * 2 + mb) * 1024, 1024)
                        P.pe(lambda e, mb=mb: e.matmul(PS[s0 + 2][:, 0:NT], lhsT=memV[l][:, mb, h * 128:(h + 1) * 128], rhs=PM[:, (h % 2) * 2 + mb, 0:NT],
                                                       start=(mb == 0), stop=(mb == 1)), r=[('memV', l)] + pk, w=[psk(s0 + 2)])
                        P.pe(lambda e, mb=mb: e.matmul(PS[s0 + 3][:, 0:NT], lhsT=ones_b[:, :], rhs=PM[:, (h % 2) * 2 + mb, 0:NT],
                                                       start=(mb == 0), stop=(mb == 1)), r=['ones_b'] + pk, w=[psk(s0 + 3)])

                scores(0)
                scores(1)
                for h in range(4):
                    pv(h)
                    if h >= 1:
                        fin(h - 1)
                    if h + 2 < 4:
                        scores(h + 2)
                fin(3)
                simple_proj(NT, f'mo{l}', 4, lambda kc: OM[:, kc, 0:NT], lambda kc: pg(o_OM + kc * 1024, 1024), l, 1)

            def ffn(l):
                o_HT, o_sg = 0, 22528
                HT = wv(o_HT, (22, 512), BF16)
                sg = [wv(o_sg + i * 2048, (512,), F32) for i in range(2)]
                for j in range(11):
                    gw, gk = wblock(f'fg{l}', j * 256, 256)
                    uw, uk = wblock(f'fu{l}', j * 256, 256)
                    fbanks = [(mmbank(), mmbank()) for _ in range(2)]
                    if j == 0:
                        accs = []
                        for mm_ in range(2):
                            accs.append((fbanks[mm_][0], (lambda kc, mm_=mm_, gw=gw: gw[:, kc, mm_ * 128:(mm_ + 1) * 128]), gk))
                            accs.append((fbanks[mm_][1], (lambda kc, mm_=mm_, uw=uw: uw[:, kc, mm_ * 128:(mm_ + 1) * 128]), uk))
                        mm_interleaved(accs)
                    for mm_ in range(2):
                        f = j * 2 + mm_
                        bg_, bu_ = fbanks[mm_]
                        if j > 0:
                            for kc in range(8):
                                P.pe(lambda e, kc=kc, mm_=mm_, bg_=bg_, gw=gw: e.matmul(PS[bg_][:, 0:NT], lhsT=gw[:, kc, mm_ * 128:(mm_ + 1) * 128], rhs=xbf[:, kc, 0:NT],
                                                                                  start=(kc == 0), stop=(kc == 7)), r=[gk, ('xbf', kc)], w=[psk(bg_)])
                            for kc in range(8):
                                P.pe(lambda e, kc=kc, mm_=mm_, bu_=bu_, uw=uw: e.matmul(PS[bu_][:, 0:NT], lhsT=uw[:, kc, mm_ * 128:(mm_ + 1) * 128], rhs=xbf[:, kc, 0:NT],
                                                                                  start=(kc == 0), stop=(kc == 7)), r=[uk, ('xbf', kc)], w=[psk(bu_)])
                        s = sg[f % 2]; sk = pg(o_sg + (f % 2) * 2048, 2048)
                        P.act(lambda e, s=s, bg_=bg_: e.activation(out=s[:, 0:NT], in_=PS[bg_][:, 0:NT], func=AF.Silu), r=[psk(bg_)], w=sk)
                        P.dve(lambda e, s=s, bu_=bu_, f=f: e.tensor_tensor(out=HT[:, f, 0:NT], in0=PS[bu_][:, 0:NT], in1=s[:, 0:NT], op=ALU.mult),
                              r=[psk(bu_)] + sk, w=pg(o_HT + f * 1024, 1024))
                    flush_fin()
                P.act(lambda e: e.activation(out=dmy[:, 0:1], in_=dmy[:, 1:2], func=AF.Ln), r=['dmy'], w=['dmy'])

                def mm(m, bank):
                    for kh in range(2):
                        wvw, wk = wblock(f'fd{l}', m * 128, 128, k0=kh * 11, nk=11)
                        for kk in range(11):
                            kc = kh * 11 + kk
                            P.pe(lambda e, wvw=wvw, kk=kk, kc=kc: e.matmul(PS[bank][:, 0:NT], lhsT=wvw[:, kk, :], rhs=HT[:, kc, 0:NT], start=(kc == 0), stop=(kc == 21)),
                                 r=[wk] + pg(o_HT + kc * 1024, 1024), w=[psk(bank)])
                proj_postnorm(NT, l, 2, mm, need_bf=(l == 0))

            if dbg <= 5:
                emit_y(); return
            mem_attn(0)
            if dbg <= 6:
                emit_y(); return
            ffn(0)
            if dbg <= 7:
                emit_y(); return

            o_UT, o_VT, o_VN, o_GT = 0, 8192, 24576, 12288
            UT = wv(o_UT, (8, 512), BF16); VT = wv(o_VT, (8, 512), F32)
            VN = wv(o_VN, (4, 1024), BF16); GT = wv(o_GT, (8, 512), BF16)
            ublks = [wblock('win1', j * 256, 256) for j in range(2)]
            ubanks = [mmbank() for _ in range(4)]
            mm_interleaved([(ubanks[h], (lambda kc, h=h: ublks[h // 2][0][:, kc, (h % 2) * 128:(h % 2) * 128 + 128]), ublks[h // 2][1]) for h in range(4)])
            for j in range(8):
                if j >= 2:
                    wvw, wk = wblock('win1', j * 256, 256)
                for mm_ in range(2):
                    cch = j * 2 + mm_
                    if j < 2:
                        bank = ubanks[cch]
                    else:
                        bank = mmbank()
                        for kc in range(8):
                            P.pe(lambda e, wvw=wvw, kc=kc, mm_=mm_, bank=bank: e.matmul(PS[bank][:, 0:NT], lhsT=wvw[:, kc, mm_ * 128:(mm_ + 1) * 128], rhs=xbf[:, kc, 0:NT],
                                                                                         start=(kc == 0), stop=(kc == 7)), r=[wk, ('xbf', kc)], w=[psk(bank)])
                    if cch < 8:
                        P.act(lambda e, cch=cch, bank=bank: e.activation(out=UT[:, cch, 0:NT], in_=PS[bank][:, 0:NT], func=AF.Gelu, bias=bincol(cch)),
                              r=[psk(bank), 'colsB'], w=pg(o_UT + cch * 1024, 1024))
                        if cch == 3:
                            flush_fin()
                    else:
                        c = cch - 8
                        P.act(lambda e, c=c, cch=cch, bank=bank: e.activation(out=VT[:, c, 0:NT], in_=PS[bank][:, 0:NT], func=AF.Gelu, bias=bincol(cch)),
                              r=[psk(bank), 'colsB'], w=pg(o_VT + c * 2048, 2048))
                        stats_ew(NT, c, VT[:, c, 0:NT], pg(o_VT + c * 2048, 2048))
                        if c > 0:
                            stats_pe(NT, c - 1)
            P.act(lambda e: e.activation(out=dmy[:, 0:1], in_=dmy[:, 1:2], func=AF.Ln), r=['dmy'], w=['dmy'])
            stats_pe(NT, 7)
            ln_apply(NT, lambda c: VT[:, c, 0:NT], None, gngcol, gnbcol, lambda c: pg(o_VT + c * 2048, 2048))
            for sub in range(nsub):
                for half in range(2):
                    bank = mmbank()
                    for q in range(4):
                        c = half * 4 + q
                        P.pe(lambda e, c=c, q=q, sub=sub, bank=bank: e.transpose(out=PS[bank][0:sw, q * 128:(q + 1) * 128], in_=VT[:, c, sub * 128:sub * 128 + sw], identity=ident[:]),
                             r=pg(o_VT + c * 2048, 2048) + ['ident'], w=[psk(bank)])
                    P.act(lambda e, sub=sub, half=half, bank=bank: e.copy(out=VN[0:sw, sub, half * 512:(half + 1) * 512], in_=PS[bank][0:sw, :]),
                          r=[psk(bank)], w=pg(o_VN + sub * 2048, 2048))
                    if sample:
                        stage_out(PS[bank][0:sw, :], sw, 512, gvs[0:64, half * 512:(half + 1) * 512], [psk(bank)])
            tg = [wv(o_VT + i * 2048, (4, 128), F32) for i in range(2)]
            for sub in range(nsub):
                for half in range(2):
                    bank = mmbank()
                    for q in range(4):
                        g = half * 4 + q
                        P.pe(lambda e, g=g, q=q, sub=sub, bank=bank: e.matmul(PS[bank][:, q * 128:q * 128 + sw], lhsT=VN[0:sw, sub, g * 128:(g + 1) * 128], rhs=wsT[0:sw, g, 0:sw],
                                                                               start=True, stop=True), r=pg(o_VN + sub * 2048, 2048) + ['wsT'], w=[psk(bank)])
                    t = tg[half]; tk = pg(o_VT + half * 2048, 2048)
                    srcv = PS[bank][:, :].rearrange("p (a b) -> p a b", a=4)[:, :, 0:sw]
                    P.dve(lambda e, t=t, srcv=srcv, half=half: e.tensor_tensor(out=t[:, :, 0:sw], in0=srcv, in1=bsb[:, half * 4:half * 4 + 4, 0:sw], op=ALU.add),
                          r=[psk(bank), 'bsb'] + [k for c in range(8) for k in pg(o_VT + c * 2048, 2048)], w=tk)
                    P.dve(lambda e, t=t, sub=sub, half=half: e.tensor_tensor(out=GT[:, half * 4:half * 4 + 4, sub * 128:sub * 128 + sw], in0=t[:, :, 0:sw],
                                                                             in1=UT[:, half * 4:half * 4 + 4, sub * 128:sub * 128 + sw], op=ALU.mult),
                          r=tk + pg(o_UT + half * 4096, 4096), w=pg(o_GT + half * 4096, 4096))
            simple_proj(NT, 'wout1', 8, lambda kc: GT[:, kc, 0:NT], lambda kc: pg(o_GT + kc * 1024, 1024), 1, 0)
            if dbg <= 8:
                emit_y(); return
            mem_attn(1)
            xprefetch(tj, sample)
            ffn(1)
            emit_y()

        for tj in range(nprompt):
            tile_pass(tj, False)
            if tj == 0:
                P.fence([('wstg', g) for g in range(NSTG)], [('V', b, par) for b in range(4, 32) for par in range(2)] + [('Vones', b) for b in range(4, 32)])
        if do_sample:
            tile_pass(0, True)
        P.finalize(out_dma_sems=[k for k in ['so0', 'so1', 'solf', 'socs'] if k in P.dma_counts])
        P.emit()
    return nc, P


def make_in_maps(inp):
    f = lambda a: np.ascontiguousarray(a, dtype=np.float32)
    shared = {
        'win0': f(inp['w_in_even'][0]), 'wout0': f(inp['w_out_even'][0]),
        'win1': f(inp['w_in_odd'][0]), 'wout1': f(inp['w_out_odd'][0]),
        'b_forget': f(inp['b_forget']).reshape(1, 8), 'conv_w': f(inp['conv_w'][0]).reshape(12, 128),
        'b_in_odd': f(inp['b_in_odd'][0]).reshape(16, 128), 'gng': f(inp['gmlp_norm_g'][0]).reshape(8, 128),
        'gnb': f(inp['gmlp_norm_b'][0]).reshape(8, 128), 'gws': f(inp['gmlp_w_s'][0]),
        'gbs': f(inp['gmlp_b_s'][0]).reshape(1, 1024),
        'ln_g': f(inp['ln_g']).reshape(48, 128), 'ln_b': f(inp['ln_b']).reshape(48, 128),
    }
    for l in range(2):
        shared[f'mq{l}'] = f(inp['mem_w_q'][l]); shared[f'mk{l}'] = f(inp['mem_w_k'][l])
        shared[f'mv{l}'] = f(inp['mem_w_v'][l]); shared[f'mo{l}'] = f(inp['mem_w_o'][l])
        shared[f'fg{l}'] = f(inp['ffn_w_gate'][l]); shared[f'fu{l}'] = f(inp['ffn_w_up'][l])
        shared[f'fd{l}'] = f(inp['ffn_w_down'][l])
    maps = []
    for b in range(8):
        m = dict(shared)
        m['xp'] = f(inp['x_prompt'][b]); m['xs'] = f(inp['x_sample'][b])
        m['cfk'] = f(inp['cache_fox_k'][0, b]).reshape(1024, 512)
        m['cfv'] = f(inp['cache_fox_v'][0, b]).reshape(1024, 512)
        m['cfl'] = f(inp['cache_fox_logf'][0, b])
        m['sconv'] = f(inp['state_conv'][0, b])
        m['cmk'] = f(inp['cache_mem_k'][:, b]).reshape(2, 256, 512)
        m['cmv'] = f(inp['cache_mem_v'][:, b]).reshape(2, 256, 512)
        m['memp'] = f(inp['mem_prompt'][b])
        maps.append(m)
    return maps


_CACHE = {}


def kernel(**inputs):
    inp = {k: np.asarray(v) for k, v in inputs.items()}
    if 'nc' not in _CACHE:
        _CACHE['nc'] = build_program()[0]
    nc = _CACHE['nc']
    maps = make_in_maps(inp)
    res = run_bass_kernel_spmd(nc, maps, core_ids=list(range(8)))
    R = res.results
    st = lambda name: np.stack([np.asarray(R[b][name], dtype=np.float32) for b in range(8)], axis=0)
    y_prompt = st('yp'); y_sample = st('ys')
    fox_k_prompt = st('fkp').reshape(1, 8, 4096, 8, 64)
    fox_v_prompt = st('fvp').reshape(1, 8, 4096, 8, 64)
    fox_logf_prompt = st('flp').reshape(1, 8, 4096, 8)
    conv_state_prompt = st('csp').reshape(1, 8, 2, 512)
    mem_k_prompt = np.transpose(st('mkp'), (1, 0, 2, 3)).reshape(2, 8, 256, 4, 128)
    mem_v_prompt = np.transpose(st('mvp'), (1, 0, 2, 3)).reshape(2, 8, 256, 4, 128)
    fox_k_sample = st('fks').reshape(1, 8, 64, 8, 64)
    fox_v_sample = st('fvs').reshape(1, 8, 64, 8, 64)
    fox_logf_sample = st('fls').reshape(1, 8, 64, 8)
    conv_state_sample = st('css').reshape(1, 8, 2, 512)
    gmlp_v_sample = st('gvs').reshape(1, 8, 64, 1024)
    return (y_prompt, y_sample, fox_k_prompt, fox_v_prompt, fox_logf_prompt, conv_state_prompt,
            mem_k_prompt, mem_v_prompt, fox_k_sample, fox_v_sample, fox_logf_sample, conv_state_sample,
            gmlp_v_sample)
```
